# Optimizing a Trainium2 kernel written in Bass

```python
import math
import jax, jax.numpy as jnp
from jax import lax
import numpy as np

D_MODEL = 2048
BATCH = 4
SEQ = 4096
DEPTH = 2

N_MIXERS = 4
GROUP_WIDTH = D_MODEL // N_MIXERS
N_HEADS_GLA = 4
N_HEADS_GDN = 4
N_POOL_GROUPS = 4
N_HEADS_FOX = 4
HEAD_DIM = GROUP_WIDTH // 4
GLA_KEY_DIM = HEAD_DIM // 2
GLA_GATE_RANK = 16
GLA_GATE_TAU = 16.0
GDN_CONV = 4
POOL_WINDOWS = (2, 4, 8, 16)
POOL_GROUP_DIM = GROUP_WIDTH // N_POOL_GROUPS
CHUNK = 64
FOX_BLOCK = 128
D_FF = 256 * int(math.ceil(8 * D_MODEL / 3 / 256))
FFN_CONV = 3
EPS = 1e-6

IN_SPLITS = (
    N_HEADS_GLA * GLA_KEY_DIM,
    N_HEADS_GLA * GLA_KEY_DIM,
    GROUP_WIDTH,
    GROUP_WIDTH,
    GLA_GATE_RANK,
    3 * GROUP_WIDTH,
    GROUP_WIDTH,
    N_HEADS_GDN,
    N_HEADS_GDN,
    GROUP_WIDTH,
    GROUP_WIDTH,
    GROUP_WIDTH,
    GROUP_WIDTH,
    N_HEADS_FOX,
)
IN_WIDTH = sum(IN_SPLITS)
SPLIT_IDX = tuple(sum(IN_SPLITS[:i + 1]) for i in range(len(IN_SPLITS) - 1))

kernel_name = "hybrid_parallel_heads_block"


def rmsnorm(x, w):
    xf = x.astype(jnp.float32)
    y = xf * lax.rsqrt(jnp.mean(xf * xf, axis=-1, keepdims=True) + EPS)
    return (y * w.astype(jnp.float32)).astype(x.dtype)


def l2norm(x):
    return x * lax.rsqrt(jnp.sum(x * x, axis=-1, keepdims=True) + EPS)


def causal_dwconv(x, w):
    width, T = w.shape[0], x.shape[1]
    xp = jnp.pad(x, ((0, 0), (width - 1, 0), (0, 0)))
    return sum(xp[:, j:j + T] * w[j] for j in range(width))


def gla_mixer(q, k, v, g_out, g_lr, w_lr, b_lr, norm_w):
    B, T, _ = q.shape
    H, dk, dv, C = N_HEADS_GLA, GLA_KEY_DIM, HEAD_DIM, CHUNK
    N = T // C
    logg = jax.nn.log_sigmoid(g_lr @ w_lr + b_lr) / GLA_GATE_TAU
    to_c = lambda t, d: jnp.transpose(t.reshape(B, N, C, H, d), (1, 0, 3, 2, 4))
    qc, kc, vc = to_c(q * dk ** -0.5, dk), to_c(k, dk), to_c(v, dv)
    bcum = jnp.cumsum(to_c(logg, dk), axis=3)
    causal = jnp.tril(jnp.ones((C, C), bool))[:, :, None]

    def step(S, inp):
        qi, ki, vi, bi = inp
        diff = bi[:, :, :, None, :] - bi[:, :, None, :, :]
        decay = jnp.exp(jnp.where(causal, diff, -jnp.inf))
        att = jnp.einsum('bhtd,bhsd,bhtsd->bhts', qi, ki, decay)
        o = jnp.einsum('bhts,bhsv->bhtv', att, vi) + jnp.einsum('bhtd,bhdv->bhtv', qi * jnp.exp(bi), S)
        blast = bi[:, :, -1:, :]
        S = S * jnp.exp(blast)[:, :, 0, :, None] + jnp.einsum('bhsd,bhsv->bhdv', ki * jnp.exp(blast - bi), vi)
        return S, o

    S0 = jnp.zeros((B, H, dk, dv), jnp.float32)
    _, o = lax.scan(step, S0, (qc, kc, vc, bcum))
    o = jnp.transpose(o, (1, 0, 3, 2, 4)).reshape(B, T, H, dv)
    o = rmsnorm(o, norm_w) * jax.nn.silu(g_out.reshape(B, T, H, dv))
    return o.reshape(B, T, H * dv)


def gated_deltanet_mixer(qkv, g_out, beta_in, a_in, conv_w, a_log, dt_bias, norm_w):
    B, T, _ = qkv.shape
    H, d, C = N_HEADS_GDN, HEAD_DIM, CHUNK
    N = T // C
    qkv = jax.nn.silu(causal_dwconv(qkv, conv_w))
    q, k, v = jnp.split(qkv, 3, axis=-1)
    q = l2norm(q.reshape(B, T, H, d)) * d ** -0.5
    k = l2norm(k.reshape(B, T, H, d))
    v = v.reshape(B, T, H, d)
    beta = jax.nn.sigmoid(beta_in)
    g = -jnp.exp(a_log) * jax.nn.softplus(a_in + dt_bias)
    to_c = lambda t: jnp.moveaxis(t.reshape(B, N, C, H, -1), 3, 1)
    qc, kc, vc = to_c(q), to_c(k), to_c(v)
    bc = to_c(beta[..., None])
    gcum = jnp.cumsum(to_c(g[..., None])[..., 0], axis=-1)
    incl = jnp.tril(jnp.ones((C, C), bool))
    strict = jnp.tril(jnp.ones((C, C), bool), -1)
    decay = jnp.exp(jnp.where(incl, gcum[..., :, None] - gcum[..., None, :], -jnp.inf))
    kbeta = kc * bc
    lmat = jnp.where(strict, jnp.einsum('bhnid,bhnjd->bhnij', kbeta, kc) * decay, 0.0)
    rhs = jnp.concatenate([vc * bc, kbeta * jnp.exp(gcum)[..., None]], axis=-1)
    sol = lax.linalg.triangular_solve(lmat + jnp.eye(C, dtype=lmat.dtype), rhs,
                                      left_side=True, lower=True, unit_diagonal=True)
    u, w = sol[..., :d], sol[..., d:]
    attn = jnp.einsum('bhnid,bhnjd->bhnij', qc, kc) * decay

    def step(S, inp):
        q_i, k_i, u_i, w_i, a_i, g_i = inp
        v_new = u_i - jnp.einsum('bhck,bhkv->bhcv', w_i, S)
        o = (jnp.einsum('bhck,bhkv->bhcv', q_i * jnp.exp(g_i)[..., None], S)
             + jnp.einsum('bhij,bhjv->bhiv', a_i, v_new))
        g_last = g_i[..., -1:]
        S = S * jnp.exp(g_last)[..., None] + jnp.einsum(
            'bhck,bhcv->bhkv', k_i * jnp.exp(g_last - g_i)[..., None], v_new)
        return S, o

    xs = tuple(jnp.moveaxis(t, 2, 0) for t in (qc, kc, u, w, attn, gcum))
    S0 = jnp.zeros((B, H, d, d), jnp.float32)
    _, o = lax.scan(step, S0, xs)
    o = jnp.moveaxis(jnp.moveaxis(o, 0, 2), 1, 3).reshape(B, T, H, d)
    o = rmsnorm(o, norm_w) * jax.nn.silu(g_out.reshape(B, T, H, d))
    return o.reshape(B, T, H * d)


def pool_mixer(u, w_grp, scale):
    B, T, Cw = u.shape
    cs = jnp.pad(jnp.cumsum(u, axis=1), ((0, 0), (1, 0), (0, 0)))
    pos = jnp.arange(1, T + 1, dtype=jnp.float32)
    outs = []
    for gi, win in enumerate(POOL_WINDOWS):
        sl = slice(gi * POOL_GROUP_DIM, (gi + 1) * POOL_GROUP_DIM)
        csg = cs[:, :, sl]
        lower = jnp.pad(csg[:, :T - win + 1], ((0, 0), (win - 1, 0), (0, 0)))
        mean = (csg[:, 1:] - lower) / jnp.minimum(pos, float(win))[None, :, None]
        outs.append(mean - u[:, :, sl])
    dlt = jnp.stack(outs, axis=2)
    y = jnp.einsum('btgc,gcd->btgd', dlt, w_grp).reshape(B, T, Cw)
    return y * scale


def forgetting_attention(q, k, v, f_in, f_bias):
    B, T, _ = q.shape
    H, d = N_HEADS_FOX, HEAD_DIM
    heads = lambda t: jnp.moveaxis(t.reshape(B, T, H, d), 2, 1)
    q, k, v = heads(q) * d ** -0.5, heads(k), heads(v)
    F = jnp.cumsum(jnp.moveaxis(jax.nn.log_sigmoid(f_in + f_bias), 2, 1), axis=-1)
    outs = []
    for i in range(T // FOX_BLOCK):
        s0, e = i * FOX_BLOCK, (i + 1) * FOX_BLOCK
        logits = (jnp.einsum('bhqd,bhkd->bhqk', q[:, :, s0:e], k[:, :, :e])
                  + F[:, :, s0:e, None] - F[:, :, None, :e])
        mask = jnp.arange(s0, e)[:, None] >= jnp.arange(e)[None, :]
        p = jax.nn.softmax(jnp.where(mask, logits, -jnp.inf), axis=-1)
        outs.append(jnp.einsum('bhqk,bhkd->bhqd', p, v[:, :, :e]))
    o = jnp.concatenate(outs, axis=2)
    return jnp.moveaxis(o, 1, 2).reshape(B, T, H * d)


def conv_ffn(h, w_gate, w_up, conv_w, conv_b, w_down):
    gate = causal_dwconv(h @ w_gate, conv_w) + conv_b
    return (jax.nn.silu(gate) * (h @ w_up)) @ w_down


def setup_inputs(seed: int = 0) -> dict:
    key = jax.random.key(seed)
    ks = iter(jax.random.split(key, 32))
    nrm = lambda shape, s: jax.random.normal(next(ks), shape, jnp.float32) * s
    L, D, F = DEPTH, D_MODEL, D_FF
    x = nrm((BATCH, SEQ, D), 1.0)
    c = nrm((BATCH, D), 1.0)
    w_mod = nrm((L, D, 6 * D), 0.5 * D ** -0.5)
    b_mod = nrm((L, 6 * D), 0.02)
    norm_mix = 1.0 + nrm((L, D), 0.1)
    norm_ffn = 1.0 + nrm((L, D), 0.1)
    w_in = nrm((L, D, IN_WIDTH), D ** -0.5)
    gla_w_lr = nrm((L, GLA_GATE_RANK, N_HEADS_GLA * GLA_KEY_DIM), GLA_GATE_RANK ** -0.5)
    gla_b_lr = nrm((L, N_HEADS_GLA * GLA_KEY_DIM), 0.1)
    gla_norm = 1.0 + nrm((L, HEAD_DIM), 0.1)
    gdn_conv = nrm((L, GDN_CONV, 3 * GROUP_WIDTH), GDN_CONV ** -0.5)
    gdn_a_log = jnp.log(jax.random.uniform(next(ks), (L, N_HEADS_GDN), jnp.float32, 1.0, 16.0))
    dt = jnp.exp(jax.random.uniform(next(ks), (L, N_HEADS_GDN), jnp.float32, math.log(1e-3), math.log(1e-1)))
    gdn_dt_bias = dt + jnp.log(-jnp.expm1(-dt))
    gdn_norm = 1.0 + nrm((L, HEAD_DIM), 0.1)
    pool_w = nrm((L, N_POOL_GROUPS, POOL_GROUP_DIM, POOL_GROUP_DIM), POOL_GROUP_DIM ** -0.5)
    pool_scale = 1.0 + nrm((L, GROUP_WIDTH), 0.1)
    fox_f_bias = 2.0 + nrm((L, N_HEADS_FOX), 0.5)
    w_out = nrm((L, D, D), D ** -0.5)
    ffn_w_gate = nrm((L, D, F), D ** -0.5)
    ffn_w_up = nrm((L, D, F), D ** -0.5)
    ffn_conv_w = nrm((L, FFN_CONV, F), FFN_CONV ** -0.5)
    ffn_conv_b = nrm((L, F), 0.02)
    ffn_w_down = nrm((L, F, D), F ** -0.5)
    norm_final = 1.0 + nrm((D,), 0.1)
    return {"x": x, "c": c, "w_mod": w_mod, "b_mod": b_mod, "norm_mix": norm_mix, "norm_ffn": norm_ffn,
            "w_in": w_in, "gla_w_lr": gla_w_lr, "gla_b_lr": gla_b_lr, "gla_norm": gla_norm,
            "gdn_conv": gdn_conv, "gdn_a_log": gdn_a_log, "gdn_dt_bias": gdn_dt_bias, "gdn_norm": gdn_norm,
            "pool_w": pool_w, "pool_scale": pool_scale, "fox_f_bias": fox_f_bias, "w_out": w_out,
            "ffn_w_gate": ffn_w_gate, "ffn_w_up": ffn_w_up, "ffn_conv_w": ffn_conv_w,
            "ffn_conv_b": ffn_conv_b, "ffn_w_down": ffn_w_down, "norm_final": norm_final}


def reference(x, c, w_mod, b_mod, norm_mix, norm_ffn, w_in, gla_w_lr, gla_b_lr, gla_norm,
              gdn_conv, gdn_a_log, gdn_dt_bias, gdn_norm, pool_w, pool_scale, fox_f_bias, w_out,
              ffn_w_gate, ffn_w_up, ffn_conv_w, ffn_conv_b, ffn_w_down, norm_final):
    B, T, D = x.shape
    f32 = lambda t: t.astype(jnp.float32)
    cond = jax.nn.silu(c)
    for l in range(DEPTH):
        mod = (cond @ w_mod[l] + b_mod[l]).reshape(B, 6, D)
        sh1, sc1, g1, sh2, sc2, g2 = [mod[:, i, None, :] for i in range(6)]

        h = rmsnorm(x, norm_mix[l]) * (1.0 + sc1) + sh1
        proj = f32(h @ w_in[l])
        (gla_q, gla_k, gla_v, gla_g, gla_lr, gdn_qkv, gdn_g, gdn_b, gdn_a,
         pool_u, fox_q, fox_k, fox_v, fox_f) = jnp.split(proj, SPLIT_IDX, axis=-1)
        y_a = gla_mixer(gla_q, gla_k, gla_v, gla_g, gla_lr, f32(gla_w_lr[l]), f32(gla_b_lr[l]), gla_norm[l])
        y_b = gated_deltanet_mixer(gdn_qkv, gdn_g, gdn_b, gdn_a, f32(gdn_conv[l]), f32(gdn_a_log[l]),
                                   f32(gdn_dt_bias[l]), gdn_norm[l])
        y_c = pool_mixer(pool_u, f32(pool_w[l]), f32(pool_scale[l]))
        y_d = forgetting_attention(fox_q, fox_k, fox_v, fox_f, f32(fox_f_bias[l]))
        y = jnp.concatenate([y_a, y_b, y_c, y_d], axis=-1).astype(x.dtype) @ w_out[l]
        x = x + g1 * y

        h = rmsnorm(x, norm_ffn[l]) * (1.0 + sc2) + sh2
        x = x + g2 * conv_ffn(h, ffn_w_gate[l], ffn_w_up[l], ffn_conv_w[l], ffn_conv_b[l], ffn_w_down[l])
    return rmsnorm(x, norm_final)
```

```python
import numpy as np
from contextlib import ExitStack
import concourse.bass as bass
import concourse.mybir as mybir
from concourse.bass_utils import run_bass_kernel_spmd
import ml_dtypes

F32 = mybir.dt.float32
BF16 = mybir.dt.bfloat16
AF = mybir.ActivationFunctionType
ALU = mybir.AluOpType
AX = mybir.AxisListType
NPBF = ml_dtypes.bfloat16

SAME_ENGINE_SYNC = True


class Buf:
    __slots__ = ("w", "r", "name")

    def __init__(self, name=""):
        self.w = None
        self.r = {}
        self.name = name


class _Eng:
    def __init__(self, name, h, sem):
        self.name, self.h, self.sem = name, h, sem
        self.count = 0
        self.waited = {}
        self.ring = []
        self.ring_pos = 0


class Ctx:
    def __init__(self, n_dma_sems=12):
        self.nc = bass.Bass("TRN2", target_bir_lowering=False)
        self.es = ExitStack()
        nc = self.nc
        self.E = {}
        for name, h in (("pe", nc.tensor), ("act", nc.scalar), ("dve", nc.vector),
                        ("pool", nc.gpsimd), ("sp", nc.sync)):
            sem = self.es.enter_context(nc.semaphore("sem_" + name))
            self.E[name] = _Eng(name, h, sem)
        for q in ("sp", "pool", "act"):
            n = n_dma_sems if q != "act" else 4
            for i in range(n):
                sem = self.es.enter_context(nc.semaphore(f"dsem_{q}{i}"))
                self.E[q].ring.append([sem, 0])
        self.n_ins = 0
        self.out_toks = []

    def sbuf(self, name, shape, dt):
        return self.es.enter_context(self.nc.sbuf_tensor(name, list(shape), dt))

    def psum(self, name, shape, dt):
        return self.es.enter_context(self.nc.psum_tensor(name, list(shape), dt))

    def dram(self, name, shape, dt, kind="Internal"):
        return self.nc.dram_tensor(name, list(shape), dt, kind=kind).ap()

    def _wait(self, e, tok):
        sem, val = tok
        if sem is e.sem and (not SAME_ENGINE_SYNC or e.name == "pe"):
            return
        key = id(sem)
        if e.waited.get(key, 0) >= val:
            return
        e.h.wait_ge(sem, val)
        e.waited[key] = val

    def _deps(self, e, reads, writes):
        for b in reads:
            if b.w is not None:
                self._wait(e, b.w)
        for b in writes:
            if b.w is not None:
                self._wait(e, b.w)
            for t in b.r.values():
                self._wait(e, t)

    def _post(self, tok, reads, writes):
        for b in writes:
            b.w = tok
            b.r = {}
        for b in reads:
            b.r[id(tok[0])] = tok

    def op(self, eng, fn, reads=(), writes=()):
        e = self.E[eng]
        self._deps(e, reads, writes)
        ins = fn(e.h)
        ins.then_inc(e.sem, 1)
        e.count += 1
        tok = (e.sem, e.count)
        self._post(tok, reads, writes)
        self.n_ins += 1
        return tok

    def dma(self, q, out, in_, reads=(), writes=(), **kw):
        e = self.E[q]
        self._deps(e, reads, writes)
        slot = e.ring[e.ring_pos]
        e.ring_pos = (e.ring_pos + 1) % len(e.ring)
        if slot[1] > 0:
            self._wait(e, (slot[0], slot[1]))
        ins = e.h.dma_start(out=out, in_=in_, **kw)
        ins.then_inc(slot[0], 16)
        slot[1] += 16
        tok = (slot[0], slot[1])
        self._post(tok, reads, writes)
        self.n_ins += 1
        return tok

    def finish(self, bufs):
        e = self.E["sp"]
        for b in bufs:
            if b.w is not None:
                self._wait(e, b.w)
        for q in ("sp", "pool", "act"):
            for slot in self.E[q].ring:
                if slot[1] > 0:
                    self._wait(e, (slot[0], slot[1]))
        self.es.close()


def load_const(C, name, shape, dt, src, q="sp"):
    t = C.sbuf("sb_" + name, shape, dt)
    b = Buf(name)
    C.dma(q, t[:], src, writes=[b])
    return t, b


def _barrier(self):
    toks = []
    for o in self.E.values():
        if o.count > 0:
            toks.append((o.sem, o.count))
        for slot in o.ring:
            if slot[1] > 0:
                toks.append((slot[0], slot[1]))
    for e in self.E.values():
        for t in toks:
            if t[0] is e.sem:
                continue
            self._wait(e, t)


def _push(self):
    self._scopes = getattr(self, "_scopes", [])
    self._scopes.append(self.es)
    self.es = ExitStack()


def _pop(self):
    self.barrier()
    self.es.close()
    self.es = self._scopes.pop()


Ctx.barrier = _barrier
Ctx.push = _push
Ctx.pop = _pop


D = 2048
NK = 16
EPS = 1e-6
LIMIT = 9

DFF = 5632
NF = 44


def _load_const(C, name, shape, dt, src, q="sp"):
    t = C.sbuf("sb_" + name, shape, dt)
    b = Buf(name)
    C.dma(q, t[:], src, writes=[b])
    return t, b


def build_stage_b(final_norm, ntiles=4):
    C = Ctx()
    NT = 128 + ntiles * 512
    x_in = C.dram("x_in", [NT, D], F32, kind="ExternalInput")
    yT_in = C.dram("yT_in", [D, NT], BF16, kind="ExternalInput")
    w_out = C.dram("w_out", [D, D], F32, kind="ExternalInput")
    w_gate = C.dram("w_gate", [D, DFF], F32, kind="ExternalInput")
    w_up = C.dram("w_up", [D, DFF], F32, kind="ExternalInput")
    w_down = C.dram("w_down", [DFF, D], F32, kind="ExternalInput")
    g1_d = C.dram("g1_b", [128, D], F32, kind="ExternalInput")
    g2_d = C.dram("g2_b", [128, D], F32, kind="ExternalInput")
    nf_d = C.dram("nf_b", [128, D], F32, kind="ExternalInput")
    fm_d = C.dram("fm_vecs", [128, 3, NK], F32, kind="ExternalInput")
    cw_d = C.dram("cw", [128, NF, 4], F32, kind="ExternalInput")
    halo_d = C.dram("halo_on", [128, 1], F32, kind="ExternalInput")
    id_d = C.dram("ident", [128, 128], BF16, kind="ExternalInput")
    x_out = C.dram("x_out", [ntiles * 512, D], F32, kind="ExternalOutput")
    xo_b = Buf("x_out")

    g1, g1b = _load_const(C, "g1", [128, D], F32, g1_d[:, :])
    g2, g2b = _load_const(C, "g2", [128, D], F32, g2_d[:, :])
    fm, fmb = _load_const(C, "fm", [128, 3, NK], F32, fm_d[:, :, :])
    cw, cwb = _load_const(C, "cwt", [128, NF, 4], F32, cw_d[:, :, :])
    halo_on, hob = _load_const(C, "halo_on", [128, 1], F32, halo_d[:, :])
    ident, idb = _load_const(C, "ident", [128, 128], BF16, id_d[:, :])
    if final_norm:
        nf, nfb = _load_const(C, "nf", [128, D], F32, nf_d[:, :])
    A2 = C.sbuf("A2", [128, NK], F32)
    A2b = Buf("A2")
    C.op("dve", lambda e: e.scalar_tensor_tensor(out=A2[:], in0=fm[:, 1, :], scalar=1.0, in1=fm[:, 0, :],
                                                  op0=ALU.add, op1=ALU.mult), reads=[fmb], writes=[A2b])

    yh = C.sbuf("yh", [128, NK, 512], BF16); yhb = Buf("yh")
    xt = C.sbuf("xt", [128, 4, D], F32); xtb = [Buf(f"xt{s}") for s in range(4)]
    xs = C.sbuf("xs", [128, D], BF16); xsb = Buf("xs")
    act = C.sbuf("act", [128, NF, 512], BF16); actb = Buf("act")
    NSLOT = 4
    wslot = [C.sbuf(f"wslot{i}", [128, NK, 512], BF16) for i in range(NSLOT)]
    wsb = [Buf(f"wslot{i}") for i in range(NSLOT)]
    wpos = [0]

    def next_slot():
        i = wpos[0] % NSLOT
        wpos[0] += 1
        return wslot[i], wsb[i]

    hal = C.sbuf("hal", [128, NF, 2], F32); halb = Buf("hal")
    gbuf = [C.sbuf(f"gbuf{i}", [128, 514], F32) for i in range(2)]; gbb = [Buf() for _ in range(2)]
    t1 = [C.sbuf(f"t1_{i}", [128, 512], F32) for i in range(2)]; t1b = [Buf() for _ in range(2)]
    sg = [C.sbuf(f"sg{i}", [128, 512], F32) for i in range(2)]; sgb = [Buf() for _ in range(2)]
    tmp = [C.sbuf(f"tmp{i}", [128, 512], F32) for i in range(2)]; tmpb = [Buf() for _ in range(2)]
    stat = C.sbuf("stat", [128, 8], F32); statb = Buf("stat")
    junk = C.sbuf("junk", [128, D], BF16); junkb = Buf("junk")
    P = [C.psum(f"P{i}", [128, 512], F32) for i in range(6)]; Pb = [Buf(f"P{i}") for i in range(6)]
    PTs = [C.psum(f"PT{i}", [128, 512], BF16) for i in range(2)]; PTb = [Buf("PT0"), Buf("PT1")]
    C.op("dve", lambda e: e.memset(hal[:], 0.0), writes=[halb])

    wo_v = w_out.rearrange("(k p) d -> p k d", p=128)
    wg_v = w_gate.rearrange("(k p) f -> p k f", p=128)
    wu_v = w_up.rearrange("(k p) f -> p k f", p=128)
    wd_v = w_down.rearrange("(c p) d -> p c d", p=128)
    yT_v = yT_in.rearrange("(k p) t -> p k t", p=128)
    cnt = {"tmp": 0, "g": 0, "pt": 0}

    def rms_rstd(src_ap, srcb, col):
        C.op("act", lambda e: e.activation(out=junk[:], in_=src_ap, func=AF.Square, accum_out=stat[:, col:col + 1]),
             reads=[srcb], writes=[junkb, statb])
        C.op("act", lambda e: e.activation(out=stat[:, col + 1:col + 2], in_=stat[:, col:col + 1], func=AF.Ln,
                                           scale=1.0 / D, bias=EPS), reads=[statb], writes=[statb])
        C.op("act", lambda e: e.activation(out=stat[:, col + 2:col + 3], in_=stat[:, col + 1:col + 2], func=AF.Exp,
                                           scale=-0.5), reads=[statb], writes=[statb])
        return stat[:, col + 2:col + 3]

    tiles = [(0, 1, True)] + [(128 + i * 512, 4, False) for i in range(ntiles)]
    jobs = []
    for ti, (tok0, ns, gate_only) in enumerate(tiles):
        for j in range(4):
            jobs.append({"kind": "wo", "tile": ti, "j": j})
        for fg in range(NF // 4):
            jobs.append({"kind": "gu", "tile": ti, "fg": fg, "gate_only": gate_only})
        if not gate_only:
            for j in range(4):
                for kg in range(4):
                    jobs.append({"kind": "wd", "tile": ti, "j": j, "kg": kg})

    def issue(job):
        k = job["kind"]
        if k == "wo":
            wt, wb = next_slot()
            C.dma("pool", wt[:], wo_v[:, :, job["j"] * 512:(job["j"] + 1) * 512], writes=[wb])
            job["w"] = [(wt, wb)]
        elif k == "gu":
            fg = job["fg"]
            wt, wb = next_slot()
            C.dma("pool", wt[:], wg_v[:, :, fg * 512:(fg + 1) * 512], writes=[wb])
            job["w"] = [(wt, wb)]
            if not job["gate_only"]:
                wt2, wb2 = next_slot()
                C.dma("pool", wt2[:], wu_v[:, :, fg * 512:(fg + 1) * 512], writes=[wb2])
                job["w"].append((wt2, wb2))
        else:
            wt, wb = next_slot()
            j, kg = job["j"], job["kg"]
            C.dma("pool", wt[:, 0:11, :], wd_v[:, kg * 11:(kg + 1) * 11, j * 512:(j + 1) * 512], writes=[wb])
            job["w"] = [(wt, wb)]

    jpos = [0]

    def take(kind):
        i = jpos[0]
        job = jobs[i]
        while job["kind"] != kind:
            jpos[0] += 1; i = jpos[0]; job = jobs[i]
        if "w" not in job:
            issue(job)
        if i + 1 < len(jobs) and LIMIT > 3:
            issue(jobs[i + 1])
        jpos[0] += 1
        return job

    for ti, (tok0, ns, gate_only) in enumerate(tiles):
        W = ns * 128
        C.dma("sp", yh[:, :, 0:W], yT_v[:, :, tok0:tok0 + W], writes=[yhb])
        for s in range(ns):
            C.dma("sp", xt[:, s, :], x_in[tok0 + s * 128:tok0 + (s + 1) * 128, :], writes=[xtb[s]])
        for j in range(4):
            job = take("wo")
            wt, wb = job["w"][0]
            for s in range(ns):
                pi = (j * ns + s) % 2
                for k in range(NK):
                    C.op("pe", lambda e: e.matmul(P[pi][:], yh[:, k, s * 128:(s + 1) * 128], wt[:, k, :],
                                                  start=(k == 0), stop=(k == NK - 1)),
                         reads=[yhb, wb], writes=[Pb[pi]])
                ti_ = cnt["tmp"] % 2; cnt["tmp"] += 1
                C.op("dve", lambda e: e.tensor_tensor(out=tmp[ti_][:], in0=P[pi][:], in1=g1[:, j * 512:(j + 1) * 512],
                                                      op=ALU.mult), reads=[Pb[pi], g1b], writes=[tmpb[ti_]])
                C.op("dve", lambda e: e.tensor_tensor(out=xt[:, s, j * 512:(j + 1) * 512],
                                                      in0=xt[:, s, j * 512:(j + 1) * 512], in1=tmp[ti_][:], op=ALU.add),
                     reads=[tmpb[ti_], xtb[s]], writes=[xtb[s]])
        if LIMIT <= 1:
            if not gate_only:
                for s in range(4):
                    C.dma("sp", x_out[tok0 - 128 + s * 128:tok0 - 128 + (s + 1) * 128, :], xt[:, s, :], reads=[xtb[s]], writes=[xo_b])
            continue
        for s in range(ns):
            rstd = rms_rstd(xt[:, s, :], xtb[s], 0)
            C.op("act", lambda e: e.activation(out=xs[:], in_=xt[:, s, :], func=AF.Copy, scale=rstd),
                 reads=[xtb[s], statb], writes=[xsb])
            for kg in range(4 if not False else 0):
                pti = cnt["pt"] % 2; cnt["pt"] += 1
                for kk in range(4):
                    k = kg * 4 + kk
                    C.op("pe", lambda e: e.transpose(PTs[pti][:, kk * 128:(kk + 1) * 128],
                                                     xs[:, k * 128:(k + 1) * 128], ident[:]),
                         reads=[xsb, idb], writes=[PTb[pti]])
                for kk in range(4):
                    k = kg * 4 + kk
                    C.op("dve", lambda e: e.tensor_scalar(out=yh[:, k, s * 128:(s + 1) * 128],
                                                          in0=PTs[pti][:, kk * 128:(kk + 1) * 128],
                                                          scalar1=A2[:, k:k + 1], scalar2=fm[:, 2, k:k + 1],
                                                          op0=ALU.mult, op1=ALU.add),
                         reads=[PTb[pti], A2b, fmb], writes=[yhb])
        if LIMIT <= 2:
            if not gate_only:
                for s in range(4):
                    C.dma("sp", x_out[tok0 - 128 + s * 128:tok0 - 128 + (s + 1) * 128, :], xt[:, s, :], reads=[xtb[s], yhb], writes=[xo_b])
            continue
        for fg in range(NF // 4):
            job = take("gu")
            wgt, wgb = job["w"][0]
            if not gate_only:
                wut, wub = job["w"][1]
            for c in range(4):
                fc = fg * 4 + c
                pg = 2 + (fc % 2)
                pu = 4 + (fc % 2)
                for k in range(NK):
                    C.op("pe", lambda e: e.matmul(P[pg][:, 0:W], wgt[:, k, c * 128:(c + 1) * 128], yh[:, k, 0:W],
                                                  start=(k == 0), stop=(k == NK - 1)),
                         reads=[yhb, wgb], writes=[Pb[pg]])
                if not gate_only:
                    for k in range(NK):
                        C.op("pe", lambda e: e.matmul(P[pu][:, 0:W], wut[:, k, c * 128:(c + 1) * 128], yh[:, k, 0:W],
                                                      start=(k == 0), stop=(k == NK - 1)),
                             reads=[yhb, wub], writes=[Pb[pu]])
                gi = cnt["g"] % 2; cnt["g"] += 1
                C.op("act", lambda e: e.copy(out=gbuf[gi][:, 0:2], in_=hal[:, fc, :]),
                     reads=[halb], writes=[gbb[gi]])
                C.op("act", lambda e: e.copy(out=gbuf[gi][:, 2:2 + W], in_=P[pg][:, 0:W]),
                     reads=[Pb[pg]], writes=[gbb[gi]])
                if gate_only:
                    C.op("act", lambda e: e.activation(out=hal[:, fc, :], in_=gbuf[gi][:, W:W + 2], func=AF.Copy,
                                                       scale=halo_on[:, 0:1]),
                         reads=[gbb[gi], hob], writes=[halb])
                    continue
                C.op("act", lambda e: e.copy(out=hal[:, fc, :], in_=gbuf[gi][:, W:W + 2]),
                     reads=[gbb[gi]], writes=[halb])
                C.op("dve", lambda e: e.tensor_scalar(out=t1[gi][:], in0=gbuf[gi][:, 0:W], scalar1=cw[:, fc, 0:1],
                                                      scalar2=cw[:, fc, 3:4], op0=ALU.mult, op1=ALU.add),
                     reads=[gbb[gi], cwb], writes=[t1b[gi]])
                C.op("dve", lambda e: e.scalar_tensor_tensor(out=t1[gi][:], in0=gbuf[gi][:, 1:W + 1],
                                                             scalar=cw[:, fc, 1:2], in1=t1[gi][:],
                                                             op0=ALU.mult, op1=ALU.add),
                     reads=[gbb[gi], cwb, t1b[gi]], writes=[t1b[gi]])
                C.op("dve", lambda e: e.scalar_tensor_tensor(out=t1[gi][:], in0=gbuf[gi][:, 2:W + 2],
                                                             scalar=cw[:, fc, 2:3], in1=t1[gi][:],
                                                             op0=ALU.mult, op1=ALU.add),
                     reads=[gbb[gi], cwb, t1b[gi]], writes=[t1b[gi]])
                C.op("act", lambda e: e.activation(out=sg[gi][:], in_=t1[gi][:], func=AF.Silu),
                     reads=[t1b[gi]], writes=[sgb[gi]])
                C.op("dve", lambda e: e.tensor_tensor(out=act[:, fc, :], in0=sg[gi][:], in1=P[pu][:], op=ALU.mult),
                     reads=[sgb[gi], Pb[pu]], writes=[actb])
        if gate_only:
            continue
        if LIMIT <= 3:
            for s in range(4):
                C.dma("sp", x_out[tok0 - 128 + s * 128:tok0 - 128 + (s + 1) * 128, :], xt[:, s, :], reads=[xtb[s], actb], writes=[xo_b])
            continue
        for j in range(4):
            for kg in range(4):
                job = take("wd")
                wt, wb = job["w"][0]
                for s in range(4):
                    for c in range(11):
                        fc = kg * 11 + c
                        C.op("pe", lambda e: e.matmul(P[s][:], act[:, fc, s * 128:(s + 1) * 128], wt[:, c, :],
                                                      start=(fc == 0), stop=(fc == NF - 1)),
                             reads=[actb, wb], writes=[Pb[s]])
            for s in range(4):
                ti_ = cnt["tmp"] % 2; cnt["tmp"] += 1
                C.op("dve", lambda e: e.tensor_tensor(out=tmp[ti_][:], in0=P[s][:], in1=g2[:, j * 512:(j + 1) * 512],
                                                      op=ALU.mult), reads=[Pb[s], g2b], writes=[tmpb[ti_]])
                C.op("dve", lambda e: e.tensor_tensor(out=xt[:, s, j * 512:(j + 1) * 512],
                                                      in0=xt[:, s, j * 512:(j + 1) * 512], in1=tmp[ti_][:], op=ALU.add),
                     reads=[tmpb[ti_], xtb[s]], writes=[xtb[s]])
        o0 = tok0 - 128
        for s in range(4):
            if final_norm:
                rstd = rms_rstd(xt[:, s, :], xtb[s], 4)
                C.op("dve", lambda e: e.scalar_tensor_tensor(out=xt[:, s, :], in0=xt[:, s, :], scalar=rstd, in1=nf[:],
                                                             op0=ALU.mult, op1=ALU.mult),
                     reads=[xtb[s], statb, nfb], writes=[xtb[s]])
            C.dma("sp", x_out[o0 + s * 128:o0 + (s + 1) * 128, :], xt[:, s, :], reads=[xtb[s]], writes=[xo_b])
    C.finish([xo_b])
    print("stage B instructions:", C.n_ins)
    return C.nc


def fm_layout(v):
    return np.ascontiguousarray(v.reshape(NK, 128).T)


def stage_b_inputs(x_tok, yT, W, mod, norm_final, halo_on):
    rep = lambda v: np.ascontiguousarray(np.broadcast_to(v[None, :], (128, v.shape[0]))).astype(np.float32)
    fmv = np.stack([fm_layout(W["norm_ffn"]), fm_layout(mod[4]), fm_layout(mod[3])], axis=1)
    cw = np.concatenate([W["ffn_conv_w"].T, W["ffn_conv_b"][:, None]], axis=1)
    cw = np.ascontiguousarray(cw.reshape(NF, 128, 4).transpose(1, 0, 2))
    return {
        "x_in": np.ascontiguousarray(x_tok), "yT_in": np.ascontiguousarray(yT),
        "w_out": W["w_out"], "w_gate": W["ffn_w_gate"], "w_up": W["ffn_w_up"], "w_down": W["ffn_w_down"],
        "g1_b": rep(mod[2]), "g2_b": rep(mod[5]), "nf_b": rep(norm_final),
        "fm_vecs": np.ascontiguousarray(fmv.astype(np.float32)), "cw": cw.astype(np.float32),
        "halo_on": np.full((128, 1), halo_on, np.float32), "ident": np.eye(128, dtype=NPBF),
    }


T = 4096
NFM = 16
NTM = 1024
TT = 512
NTILE = T // TT
POOL_WINDOWS = (2, 4, 8, 16)


def core_cols(hh):
    h0 = 2 * hh
    fm = -np.ones((NFM, 128), np.int64)
    fm[0] = np.arange(0 + h0 * 64, 0 + h0 * 64 + 128)
    fm[1] = np.arange(256 + h0 * 64, 256 + h0 * 64 + 128)
    for i, base in enumerate((1552, 2064, 2576)):
        fm[2 + 2 * i] = np.arange(base + h0 * 128, base + h0 * 128 + 128)
        fm[3 + 2 * i] = np.arange(base + (h0 + 1) * 128, base + (h0 + 1) * 128 + 128)
    fm[8] = np.arange(3608 + h0 * 128, 3608 + h0 * 128 + 128)
    fm[9] = np.arange(3608 + (h0 + 1) * 128, 3608 + (h0 + 1) * 128 + 128)
    fm[10] = np.arange(4120 + h0 * 128, 4120 + h0 * 128 + 128)
    fm[11] = np.arange(4120 + (h0 + 1) * 128, 4120 + (h0 + 1) * 128 + 128)
    fm[12] = np.arange(4632 + h0 * 128, 4632 + h0 * 128 + 128)
    fm[13] = np.arange(4632 + (h0 + 1) * 128, 4632 + (h0 + 1) * 128 + 128)
    fm[14, 0:16] = np.arange(1536, 1552)
    fm[14, 32:34] = [3600 + h0, 3600 + h0 + 1]
    fm[14, 64:66] = [3604 + h0, 3604 + h0 + 1]
    fm[15, 0:2] = [5656 + h0, 5656 + h0 + 1]
    tm = np.concatenate([np.arange(512 + h0 * 128, 512 + h0 * 128 + 256),
                         np.arange(1024 + h0 * 128, 1024 + h0 * 128 + 256),
                         np.arange(3088 + h0 * 128, 3088 + h0 * 128 + 256),
                         np.arange(5144 + h0 * 128, 5144 + h0 * 128 + 256)])
    return fm, tm


def arrange_w_in(w_in, hh):
    fm, tm = core_cols(hh)
    cols = np.concatenate([fm.reshape(-1), tm])
    out = np.zeros((D, cols.shape[0]), np.float32)
    ok = cols >= 0
    out[:, ok] = w_in[:, cols[ok]]
    return out


class StageA:
    def __init__(self, parts, debug_proj=False):
        self.parts = parts
        C = self.C = Ctx()
        self.x_in = C.dram("x_in", [T, D], F32, kind="ExternalInput")
        self.w_in = C.dram("w_in_core", [D, NFM * 128 + NTM], F32, kind="ExternalInput")
        self.fm1_d = C.dram("fm1", [128, 3, NK], F32, kind="ExternalInput")
        self.id_d = C.dram("ident", [128, 128], BF16, kind="ExternalInput")
        self.idf_d = C.dram("identf", [128, 128], F32, kind="ExternalInput")
        pk = "ExternalInput" if "inproj" not in parts else ("ExternalOutput" if debug_proj else "Internal")
        self.projFM = C.dram("projFM", [NFM, 128, T], F32, kind=pk)
        self.projTM = C.dram("projTM", [T, NTM], F32, kind=pk)
        self.fmb = [Buf(f"projFM{c}") for c in range(NFM)]
        self.tmb = Buf("projTM")
        self.yT = C.dram("yT", [1024, T], BF16, kind="ExternalOutput")
        self.yTb = [Buf(f"yT{m}") for m in range(4)]
        self.ident, self.idb = load_const(C, "ident", [128, 128], BF16, self.id_d[:, :])
        self.identf, self.idfb = load_const(C, "identf", [128, 128], F32, self.idf_d[:, :])
        self.P = [C.psum(f"P{i}", [128, 512], F32) for i in range(6)]
        self.Pb = [Buf(f"P{i}") for i in range(6)]
        self.PT = [C.psum(f"PT{i}", [128, 512], BF16) for i in range(2)]
        self.PTb = [Buf("PT0"), Buf("PT1")]
        self.outs = []

    def inproj(self):
        C = self.C
        C.push()
        P, Pb, PT, PTb = self.P, self.Pb, self.PT, self.PTb
        fm, fmbuf = load_const(C, "fm1", [128, 3, NK], F32, self.fm1_d[:, :, :])
        A1 = C.sbuf("A1", [128, NK], F32); A1b = Buf("A1")
        C.op("dve", lambda e: e.scalar_tensor_tensor(out=A1[:], in0=fm[:, 1, :], scalar=1.0, in1=fm[:, 0, :],
                                                      op0=ALU.add, op1=ALU.mult), reads=[fmbuf], writes=[A1b])
        NC = NFM * 128 + NTM
        wsb = C.sbuf("w_in_sb", [128, NK, NC], BF16); wb = Buf("w_in_sb")
        wv = self.w_in.rearrange("(k p) c -> p k c", p=128)
        for q in range(4):
            C.dma("pool", wsb[:, q * 4:(q + 1) * 4, :], wv[:, q * 4:(q + 1) * 4, :], writes=[wb])
        xt = C.sbuf("xt", [128, 4, D], F32); xtb = [Buf(f"xt{s}") for s in range(4)]
        xs = C.sbuf("xs", [128, D], BF16); xsb = Buf("xs")
        hT = C.sbuf("hT", [128, NK, TT], BF16); hTb = Buf("hT")
        stat = C.sbuf("stat", [128, 8], F32); statb = Buf("stat")
        junk = C.sbuf("junk", [128, D], BF16); junkb = Buf("junk")
        ev = [C.sbuf(f"ev{i}", [128, 512], F32) for i in range(3)]; evb = [Buf() for _ in range(3)]
        n_ev = 0; n_pt = 0; n_p = 0
        for n in range(NTILE):
            tok0 = n * TT
            for s in range(4):
                C.dma("sp", xt[:, s, :], self.x_in[tok0 + s * 128:tok0 + (s + 1) * 128, :], writes=[xtb[s]])
            for s in range(4):
                C.op("act", lambda e: e.activation(out=junk[:], in_=xt[:, s, :], func=AF.Square, accum_out=stat[:, 0:1]),
                     reads=[xtb[s]], writes=[junkb, statb])
                C.op("act", lambda e: e.activation(out=stat[:, 1:2], in_=stat[:, 0:1], func=AF.Ln, scale=1.0 / D, bias=EPS),
                     reads=[statb], writes=[statb])
                C.op("act", lambda e: e.activation(out=stat[:, 2:3], in_=stat[:, 1:2], func=AF.Exp, scale=-0.5),
                     reads=[statb], writes=[statb])
                C.op("act", lambda e: e.activation(out=xs[:], in_=xt[:, s, :], func=AF.Copy, scale=stat[:, 2:3]),
                     reads=[xtb[s], statb], writes=[xsb])
                for kg in range(4):
                    pti = n_pt % 2; n_pt += 1
                    for kk in range(4):
                        k = kg * 4 + kk
                        C.op("pe", lambda e: e.transpose(PT[pti][:, kk * 128:(kk + 1) * 128], xs[:, k * 128:(k + 1) * 128],
                                                         self.ident[:]), reads=[xsb, self.idb], writes=[PTb[pti]])
                    for kk in range(4):
                        k = kg * 4 + kk
                        C.op("dve", lambda e: e.tensor_scalar(out=hT[:, k, s * 128:(s + 1) * 128],
                                                              in0=PT[pti][:, kk * 128:(kk + 1) * 128],
                                                              scalar1=A1[:, k:k + 1], scalar2=fm[:, 2, k:k + 1],
                                                              op0=ALU.mult, op1=ALU.add),
                             reads=[PTb[pti], A1b, fmbuf], writes=[hTb])
            for c in range(NFM):
                pi = n_p % 4; n_p += 1
                for k in range(NK):
                    C.op("pe", lambda e: e.matmul(P[pi][:], wsb[:, k, c * 128:(c + 1) * 128], hT[:, k, :],
                                                  start=(k == 0), stop=(k == NK - 1)), reads=[wb, hTb], writes=[Pb[pi]])
                ei = n_ev % 3; n_ev += 1
                eng = "act" if c % 2 == 0 else "dve"
                if eng == "act":
                    C.op("act", lambda e: e.copy(out=ev[ei][:], in_=P[pi][:]), reads=[Pb[pi]], writes=[evb[ei]])
                else:
                    C.op("dve", lambda e: e.tensor_copy(out=ev[ei][:], in_=P[pi][:]), reads=[Pb[pi]], writes=[evb[ei]])
                C.dma("sp", self.projFM[c, :, tok0:tok0 + TT], ev[ei][:], reads=[evb[ei]], writes=[self.fmb[c]])
            for s in range(4):
                for g in range(2):
                    pi = n_p % 4; n_p += 1
                    for k in range(NK):
                        C.op("pe", lambda e: e.matmul(P[pi][:], hT[:, k, s * 128:(s + 1) * 128],
                                                      wsb[:, k, NFM * 128 + g * 512:NFM * 128 + (g + 1) * 512],
                                                      start=(k == 0), stop=(k == NK - 1)), reads=[wb, hTb], writes=[Pb[pi]])
                    ei = n_ev % 3; n_ev += 1
                    if g == 0:
                        C.op("act", lambda e: e.copy(out=ev[ei][:], in_=P[pi][:]), reads=[Pb[pi]], writes=[evb[ei]])
                    else:
                        C.op("dve", lambda e: e.tensor_copy(out=ev[ei][:], in_=P[pi][:]), reads=[Pb[pi]], writes=[evb[ei]])
                    C.dma("sp", self.projTM[tok0 + s * 128:tok0 + (s + 1) * 128, g * 512:(g + 1) * 512], ev[ei][:],
                          reads=[evb[ei]], writes=[self.tmb])
        self.outs += self.fmb + [self.tmb]
        C.pop()

    def finish(self):
        self.C.finish(self.outs + self.yTb)
        print("stage A instructions:", self.C.n_ins)
        return self.C.nc


def stage_a_base_inputs(x_b, W, mod, hh):
    fm_layout = lambda v: np.ascontiguousarray(v.reshape(NK, 128).T)
    fm1 = np.stack([fm_layout(W["norm_mix"]), fm_layout(mod[1]), fm_layout(mod[0])], axis=1).astype(np.float32)
    return {"x_in": np.ascontiguousarray(x_b), "w_in_core": arrange_w_in(W["w_in"], hh),
            "fm1": np.ascontiguousarray(fm1), "ident": np.eye(128, dtype=NPBF), "identf": np.eye(128, dtype=np.float32)}


def pool_mixer(S):
    C = S.C
    C.push()
    P, Pb = S.P, S.Pb
    pw_d = C.dram("pool_w", [128, 2, 128], F32, kind="ExternalInput")
    pc_d = C.dram("pool_c", [128, 2, 8], F32, kind="ExternalInput")
    pk_d = C.dram("pool_corr", [128, 2, 4, 16], F32, kind="ExternalInput")
    pw32, pw32b = load_const(C, "pool_w32", [128, 2, 128], F32, pw_d[:, :, :])
    pc, pcb = load_const(C, "pool_c", [128, 2, 8], F32, pc_d[:, :, :])
    pk, pkb = load_const(C, "pool_corr", [128, 2, 4, 16], F32, pk_d[:, :, :, :])
    pw = C.sbuf("pool_wbf", [128, 2, 128], BF16); pwb = Buf()
    C.op("dve", lambda e: e.tensor_copy(out=pw[:], in_=pw32[:]), reads=[pw32b], writes=[pwb])
    ub = [C.sbuf(f"ub{i}", [128, 16 + TT], F32) for i in range(2)]; ubb = [Buf() for _ in range(2)]
    s2 = C.sbuf("s2", [128, 16 + TT], F32); s4 = C.sbuf("s4", [128, 16 + TT], F32)
    s8 = C.sbuf("s8", [128, 16 + TT], F32); s16 = C.sbuf("s16", [128, 16 + TT], F32)
    sb_ = Buf("pool_s")
    acc = C.sbuf("pacc", [128, TT], F32); accb = Buf()
    tmpc = C.sbuf("ptmpc", [128, 16], F32)
    dl = C.sbuf("pdl", [128, TT], BF16); dlb = Buf()
    yo = [C.sbuf(f"pyo{i}", [128, TT], BF16) for i in range(2)]; yob = [Buf() for _ in range(2)]
    L = 16 + TT
    n = 0
    for gs in range(2):
        for t in range(NTILE):
            tok0 = t * TT
            u = ub[n % 2]; ubf = ubb[n % 2]
            if t == 0:
                C.op("dve", lambda e: e.memset(u[:, 0:16], 0.0), writes=[ubf])
                C.dma("sp", u[:, 16:L], S.projFM[8 + gs, :, 0:TT], reads=[S.fmb[8 + gs]], writes=[ubf])
            else:
                C.dma("sp", u[:, :], S.projFM[8 + gs, :, tok0 - 16:tok0 + TT], reads=[S.fmb[8 + gs]], writes=[ubf])
            eng = "dve"
            C.op(eng, lambda e: e.memset(s2[:, 0:1], 0.0), writes=[sb_])
            C.op(eng, lambda e: e.tensor_tensor(out=s2[:, 1:L], in0=u[:, 1:L], in1=u[:, 0:L - 1], op=ALU.add),
                 reads=[ubf], writes=[sb_])
            C.op(eng, lambda e: e.tensor_tensor(out=s4[:, 3:L], in0=s2[:, 3:L], in1=s2[:, 1:L - 2], op=ALU.add),
                 reads=[sb_], writes=[sb_])
            C.op(eng, lambda e: e.tensor_tensor(out=s8[:, 7:L], in0=s4[:, 7:L], in1=s4[:, 3:L - 4], op=ALU.add),
                 reads=[sb_], writes=[sb_])
            C.op(eng, lambda e: e.tensor_tensor(out=s16[:, 15:L], in0=s8[:, 15:L], in1=s8[:, 7:L - 8], op=ALU.add),
                 reads=[sb_], writes=[sb_])
            ss = [s2, s4, s8, s16]
            C.op("dve", lambda e: e.tensor_scalar(out=acc[:], in0=s2[:, 16:L], scalar1=pc[:, gs, 0:1], scalar2=None,
                                                  op0=ALU.mult), reads=[sb_, pcb], writes=[accb])
            for wi in range(1, 4):
                C.op("dve", lambda e: e.scalar_tensor_tensor(out=acc[:], in0=ss[wi][:, 16:L], scalar=pc[:, gs, wi:wi + 1],
                                                             in1=acc[:], op0=ALU.mult, op1=ALU.add),
                     reads=[sb_, pcb, accb], writes=[accb])
            if t == 0:
                C.op("dve", lambda e: e.tensor_tensor(out=acc[:, 0:16], in0=s2[:, 16:32], in1=pk[:, gs, 0, :], op=ALU.mult),
                     reads=[sb_, pkb, accb], writes=[accb])
                for wi in range(1, 4):
                    C.op("dve", lambda e: e.tensor_tensor(out=tmpc[:], in0=ss[wi][:, 16:32], in1=pk[:, gs, wi, :], op=ALU.mult),
                         reads=[sb_, pkb, accb], writes=[accb])
                    C.op("dve", lambda e: e.tensor_tensor(out=acc[:, 0:16], in0=acc[:, 0:16], in1=tmpc[:], op=ALU.add),
                         reads=[accb], writes=[accb])
            C.op("dve", lambda e: e.tensor_tensor(out=dl[:], in0=acc[:], in1=u[:, 16:L], op=ALU.subtract),
                 reads=[accb, ubf], writes=[dlb])
            pi = n % 2
            C.op("pe", lambda e: e.matmul(P[pi][:], pw[:, gs, :], dl[:], start=True, stop=True),
                 reads=[pwb, dlb], writes=[Pb[pi]])
            C.op("act", lambda e: e.activation(out=yo[n % 2][:], in_=P[pi][:], func=AF.Copy, scale=pc[:, gs, 4:5]),
                 reads=[Pb[pi], pcb], writes=[yob[n % 2]])
            C.dma("sp", S.yT[512 + gs * 128:512 + (gs + 1) * 128, tok0:tok0 + TT], yo[n % 2][:], reads=[yob[n % 2]],
                  writes=[S.yTb[2]])
            n += 1
    C.pop()


def pool_inputs(W, hh):
    pw = np.zeros((128, 2, 128), np.float32); pc = np.zeros((128, 2, 8), np.float32)
    pk = np.zeros((128, 2, 4, 16), np.float32)
    pos = np.arange(1, 17, dtype=np.float32)
    for gs in range(2):
        g = 2 * hh + gs
        pw[:, gs, :] = W["pool_w"][g]
        for wi, w in enumerate(POOL_WINDOWS):
            sel = 1.0 if wi == g else 0.0
            pc[:, gs, wi] = sel / w
            pk[:, gs, wi, :] = sel / np.minimum(pos, float(w))[None, :]
        pc[:, gs, 4] = W["pool_scale"][g * 128:(g + 1) * 128]
    return {"pool_w": pw, "pool_c": pc, "pool_corr": pk}


def fox_mixer(S):
    C = S.C
    C.push()
    P, Pb, PT, PTb = S.P, S.Pb, S.PT, S.PTb
    HD = 128
    NB = T // 128
    fc_d = C.dram("fox_c", [128, 4], F32, kind="ExternalInput")
    sel_d = C.dram("fox_sel", [128, 2, 128], F32, kind="ExternalInput")
    mk_d = C.dram("fox_mask", [128, 128], BF16, kind="ExternalInput")
    fcn, fcb = load_const(C, "fox_c", [128, 4], F32, fc_d[:, :])
    sel, selb = load_const(C, "fox_sel", [128, 2, 128], F32, sel_d[:, :, :])
    maskU, mkb = load_const(C, "fox_mask", [128, 128], BF16, mk_d[:, :])
    nfb = C.sbuf("fox_nfb", [128, 1], F32); nfbb = Buf()
    C.op("dve", lambda e: e.tensor_scalar(out=nfb[:], in0=fcn[:, 0:1], scalar1=-1.0, scalar2=None, op0=ALU.mult),
         reads=[fcb], writes=[nfbb])
    fr = C.sbuf("fox_fr", [2, T], F32); frb = Buf()
    one_r = C.sbuf("fox_ones", [2, T], F32); oneb = Buf()
    C.dma("sp", fr[:], S.projFM[15, 0:2, :], reads=[S.fmb[15]], writes=[frb])
    C.op("dve", lambda e: e.memset(one_r[:], 1.0), writes=[oneb])
    C.op("act", lambda e: e.activation(out=fr[:], in_=fr[:], func=AF.Exp, scale=-1.0, bias=nfb[0:2, 0:1]),
         reads=[frb, nfbb], writes=[frb])
    C.op("act", lambda e: e.activation(out=fr[:], in_=fr[:], func=AF.Ln, bias=1.0), reads=[frb], writes=[frb])
    G = C.sbuf("fox_G", [2, T], F32); Gb_ = Buf()
    C.op("dve", lambda e: e.tensor_tensor_scan(out=G[:], data0=one_r[:], data1=fr[:], initial=0.0, op0=ALU.mult, op1=ALU.add),
         reads=[frb, oneb], writes=[Gb_])
    Gbc = [C.sbuf(f"fox_Gbc{h}", [128, T], F32) for h in range(2)]; Gbcb = [Buf() for _ in range(2)]
    for h in range(2):
        for t in range(NTILE):
            pi = (h * NTILE + t) % 2
            C.op("pe", lambda e: e.matmul(P[pi][:], sel[0:2, h, :], G[0:2, t * TT:(t + 1) * TT], start=True, stop=True),
                 reads=[selb, Gb_], writes=[Pb[pi]])
            C.op("act", lambda e: e.copy(out=Gbc[h][:, t * TT:(t + 1) * TT], in_=P[pi][:]), reads=[Pb[pi]], writes=[Gbcb[h]])
    tmp3 = C.sbuf("fox_tmp3", [128, NB, 128], F32); tmp3b = Buf()
    Gcol = [C.sbuf(f"fox_Gcol{h}", [128, NB], F32) for h in range(2)]; Gcolb = [Buf() for _ in range(2)]
    for h in range(2):
        C.op("dve", lambda e: e.tensor_tensor(out=tmp3[:], in0=Gbc[h][:].rearrange("p (j s) -> p j s", s=128),
                                              in1=S.identf[:].unsqueeze(1).to_broadcast([128, NB, 128]), op=ALU.mult),
             reads=[Gbcb[h], S.idfb], writes=[tmp3b])
        C.op("dve", lambda e: e.tensor_reduce(out=Gcol[h][:], in_=tmp3[:], axis=AX.X, op=ALU.add),
             reads=[tmp3b], writes=[Gcolb[h]])
    kT = [C.sbuf(f"fox_kT{h}", [128, T], BF16) for h in range(2)]; kTb = [Buf() for _ in range(2)]
    qT = [C.sbuf(f"fox_qT{h}", [128, T], BF16) for h in range(2)]; qTb = [Buf() for _ in range(2)]
    Va = [C.sbuf(f"fox_Va{h}", [128, NB, 130], BF16) for h in range(2)]; Vab = [Buf() for _ in range(2)]
    for h in range(2):
        C.dma("pool", kT[h][:], S.projFM[12 + h, :, :], reads=[S.fmb[12 + h]], writes=[kTb[h]])
        C.dma("pool", qT[h][:], S.projFM[10 + h, :, :], reads=[S.fmb[10 + h]], writes=[qTb[h]])
        C.op("dve", lambda e: e.memset(Va[h][:, :, 128:130], 1.0), writes=[Vab[h]])
        C.dma("pool", Va[h][:, :, 0:128],
              S.projTM[:, 768 + h * 128:768 + (h + 1) * 128].rearrange("(j p) d -> p j d", p=128),
              reads=[S.tmb], writes=[Vab[h]])
    pt = [C.sbuf(f"fox_pt{i}", [128, 128], BF16) for i in range(4)]; ptb = [Buf() for _ in range(4)]
    bm = [C.sbuf(f"fox_bm{i}", [128, NB], F32) for i in range(2)]; bmb = [Buf() for _ in range(2)]
    rinv = C.sbuf("fox_rinv", [128, 2], F32); rinvb = Buf()
    on = [C.sbuf(f"fox_on{i}", [128, 128], BF16) for i in range(2)]; onb = [Buf() for _ in range(2)]
    yst = [C.sbuf(f"fox_yst{i}", [128, 512], BF16) for i in range(2)]; ystb = [Buf() for _ in range(2)]
    scale = float(HD) ** -0.5
    npt = 0; nps = 0; nblk = 0
    for h in range(2):
        for i in range(NB):
            b_ = bm[nblk % 2]; bb_ = bmb[nblk % 2]
            C.op("dve", lambda e: e.tensor_scalar(out=b_[:, 0:i + 1], in0=Gcol[h][:, 0:i + 1],
                                                  scalar1=Gbc[h][:, i * 128 + 127:i * 128 + 128], scalar2=None,
                                                  op0=ALU.subtract), reads=[Gcolb[h], Gbcb[h]], writes=[bb_])
            po = 4 + (nblk % 2)
            for j in range(i + 1):
                ps = nps % 4; nps += 1
                C.op("pe", lambda e: e.matmul(P[ps][:, 0:128], kT[h][:, j * 128:(j + 1) * 128], qT[h][:, i * 128:(i + 1) * 128],
                                              start=True, stop=True), reads=[kTb[h], qTb[h]], writes=[Pb[ps]])
                pi = npt % 4; npt += 1
                C.op("act", lambda e: e.activation(out=pt[pi][:], in_=P[ps][:, 0:128], func=AF.Exp, scale=scale,
                                                   bias=b_[:, j:j + 1]), reads=[Pb[ps], bb_], writes=[ptb[pi]])
                if j == i:
                    C.op("dve", lambda e: e.tensor_tensor(out=pt[pi][:], in0=pt[pi][:], in1=maskU[:], op=ALU.mult),
                         reads=[ptb[pi], mkb], writes=[ptb[pi]])
                C.op("pe", lambda e: e.matmul(P[po][:, 0:130], pt[pi][:], Va[h][:, j, :], start=(j == 0), stop=(j == i)),
                     reads=[ptb[pi], Vab[h]], writes=[Pb[po]])
            C.op("dve", lambda e: e.reciprocal(out=rinv[:, 0:1], in_=P[po][:, 128:129]), reads=[Pb[po]], writes=[rinvb])
            oi = nblk % 2
            C.op("act", lambda e: e.activation(out=on[oi][:], in_=P[po][:, 0:128], func=AF.Copy, scale=rinv[:, 0:1]),
                 reads=[Pb[po], rinvb], writes=[onb[oi]])
            pti = nblk % 2
            C.op("pe", lambda e: e.transpose(PT[pti][:, 0:128], on[oi][:], S.ident[:]), reads=[onb[oi], S.idb],
                 writes=[PTb[pti]])
            ys = (nblk // 4) % 2
            C.op("dve", lambda e: e.tensor_copy(out=yst[ys][:, (i % 4) * 128:(i % 4 + 1) * 128], in_=PT[pti][:, 0:128]),
                 reads=[PTb[pti]], writes=[ystb[ys]])
            if i % 4 == 3:
                C.dma("sp", S.yT[768 + h * 128:768 + (h + 1) * 128, (i - 3) * 128:(i + 1) * 128], yst[ys][:],
                      reads=[ystb[ys]], writes=[S.yTb[3]])
            nblk += 1
    C.pop()


def fox_inputs(W, hh):
    fc = np.zeros((128, 4), np.float32)
    fc[0, 0] = W["fox_f_bias"][2 * hh]; fc[1, 0] = W["fox_f_bias"][2 * hh + 1]
    sel = np.zeros((128, 2, 128), np.float32)
    sel[0, 0, :] = 1.0; sel[1, 1, :] = 1.0
    s = np.arange(128)
    mask = (s[:, None] <= s[None, :]).astype(NPBF)
    return {"fox_c": fc, "fox_sel": sel, "fox_mask": mask}


def gla_mixer(S):
    C = S.C
    C.push()
    P, Pb = S.P, S.Pb
    DK, DV, CH = 64, 128, 64
    wlr_d = C.dram("gla_wlr", [16, 128], F32, kind="ExternalInput")
    gc_d = C.dram("gla_c", [64, 4], F32, kind="ExternalInput")
    nw_d = C.dram("gla_nw2", [64, 256], F32, kind="ExternalInput")
    cm_d = C.dram("gla_cmask", [64, TT], F32, kind="ExternalInput")
    mk_d = C.dram("gla_mask", [64, 128], F32, kind="ExternalInput")
    wlr, wlrb = load_const(C, "gla_wlr", [16, 128], F32, wlr_d[:, :])
    gcn, gcb = load_const(C, "gla_c", [64, 4], F32, gc_d[:, :])
    nw2, nw2b = load_const(C, "gla_nw2", [64, 256], F32, nw_d[:, :])
    cmask, cmb = load_const(C, "gla_cmask", [64, TT], F32, cm_d[:, :])
    maskU2, mkb = load_const(C, "gla_mask", [64, 128], F32, mk_d[:, :])
    nb = C.sbuf("gla_nb", [64, 2], F32); nbb = Buf()
    C.op("dve", lambda e: e.tensor_scalar(out=nb[:], in0=gcn[:, 0:2], scalar1=-1.0, scalar2=None, op0=ALU.mult),
         reads=[gcb], writes=[nbb])
    Sf = C.sbuf("gla_S", [64, 2, DV], F32); Sfb = Buf()
    Sbf = C.sbuf("gla_Sbf", [64, 2, DV], BF16); Sbfb = Buf()
    tmpS = C.sbuf("gla_tmpS", [64, 2, DV], F32); tmpSb = Buf()
    C.op("dve", lambda e: e.memset(Sf[:], 0.0), writes=[Sfb])
    C.op("dve", lambda e: e.memset(Sbf[:], 0.0), writes=[Sbfb])
    lr = C.sbuf("gla_lr", [16, TT], F32); lrb = Buf()
    mk = lambda nm, dt: ([C.sbuf(f"gla_{nm}{h}", [64, TT], dt) for h in range(2)], [Buf() for _ in range(2)])
    l1, l1b = mk("l1", F32); cs, csb = mk("cs", F32); eP, ePb = mk("eP", F32); eN, eNb = mk("eN", F32)
    qf, qfb = mk("qf", F32); kf, kfb = mk("kf", F32); qe, qeb = mk("qe", BF16); ke, keb = mk("ke", BF16)
    vt = C.sbuf("gla_vt", [64, 8, 256], BF16); vtb = Buf()
    gt = C.sbuf("gla_gt", [64, 8, 256], F32); gtb = Buf()
    nwg = C.sbuf("gla_nwg", [64, 8, 256], F32); nwgb = Buf()
    ketm = C.sbuf("gla_ketm", [64, 128], BF16); ketmb = Buf()
    att = C.sbuf("gla_att", [64, 128], BF16); attb = Buf()
    st = C.sbuf("gla_st", [64, 8], F32); stb = Buf()
    junk = C.sbuf("gla_junk", [64, 128], F32); junkb = Buf()
    ytm = C.sbuf("gla_ytm", [64, 256], BF16); ytmb = Buf()
    yst = [C.sbuf(f"gla_yst{h}", [128, TT], BF16) for h in range(2)]; ystb = [Buf() for _ in range(2)]
    nchunk = 0
    for t in range(NTILE):
        tok0 = t * TT
        C.dma("sp", lr[:], S.projFM[14, 0:16, tok0:tok0 + TT], reads=[S.fmb[14]], writes=[lrb])
        for h in range(2):
            C.dma("sp", qf[h][:], S.projFM[0, h * 64:(h + 1) * 64, tok0:tok0 + TT], reads=[S.fmb[0]], writes=[qfb[h]])
            C.dma("sp", kf[h][:], S.projFM[1, h * 64:(h + 1) * 64, tok0:tok0 + TT], reads=[S.fmb[1]], writes=[kfb[h]])
        C.dma("pool", vt[:], S.projTM[tok0:tok0 + TT, 0:256].rearrange("(c p) d -> p c d", p=64), reads=[S.tmb], writes=[vtb])
        C.dma("sp", gt[:], S.projTM[tok0:tok0 + TT, 256:512].rearrange("(c p) d -> p c d", p=64), reads=[S.tmb], writes=[gtb])
        for h in range(2):
            C.op("pe", lambda e: e.matmul(P[0][0:64, :], wlr[0:16, h * 64:(h + 1) * 64], lr[0:16, :], start=True, stop=True),
                 reads=[wlrb, lrb], writes=[Pb[0]])
            C.op("act", lambda e: e.activation(out=l1[h][:], in_=P[0][0:64, :], func=AF.Exp, scale=-1.0, bias=nb[:, h:h + 1]),
                 reads=[Pb[0], nbb], writes=[l1b[h]])
            C.op("act", lambda e: e.activation(out=l1[h][:], in_=l1[h][:], func=AF.Ln, bias=1.0), reads=[l1b[h]], writes=[l1b[h]])
            C.op("dve", lambda e: e.tensor_tensor_scan(out=cs[h][:], data0=cmask[:], data1=l1[h][:], initial=0.0,
                                                       op0=ALU.mult, op1=ALU.add), reads=[l1b[h], cmb], writes=[csb[h]])
            C.op("act", lambda e: e.activation(out=eP[h][:], in_=cs[h][:], func=AF.Exp, scale=-1.0 / 16.0),
                 reads=[csb[h]], writes=[ePb[h]])
            C.op("act", lambda e: e.activation(out=eN[h][:], in_=cs[h][:], func=AF.Exp, scale=1.0 / 16.0),
                 reads=[csb[h]], writes=[eNb[h]])
            C.op("dve", lambda e: e.scalar_tensor_tensor(out=qe[h][:], in0=qf[h][:], scalar=float(DK) ** -0.5, in1=eP[h][:],
                                                         op0=ALU.mult, op1=ALU.mult), reads=[qfb[h], ePb[h]], writes=[qeb[h]])
            C.op("dve", lambda e: e.tensor_tensor(out=ke[h][:], in0=kf[h][:], in1=eN[h][:], op=ALU.mult),
                 reads=[kfb[h], eNb[h]], writes=[keb[h]])
        C.op("act", lambda e: e.activation(out=gt[:], in_=gt[:], func=AF.Silu), reads=[gtb], writes=[gtb])
        C.op("dve", lambda e: e.tensor_tensor(out=nwg[:], in0=gt[:], in1=nw2[:].unsqueeze(1).to_broadcast([64, 8, 256]),
                                              op=ALU.mult), reads=[gtb, nw2b], writes=[nwgb])
        for c in range(8):
            cols = slice(c * CH, (c + 1) * CH)
            for h in range(2):
                C.op("pe", lambda e: e.matmul(P[5][0:64, h * 64:(h + 1) * 64], ke[h][:, cols], S.ident[0:64, 0:64],
                                              start=True, stop=True), reads=[keb[h], S.idb], writes=[Pb[5]])
            C.op("act", lambda e: e.copy(out=ketm[:], in_=P[5][0:64, 0:128]), reads=[Pb[5]], writes=[ketmb])
            for h in range(2):
                C.op("pe", lambda e: e.matmul(P[1][0:64, h * 64:(h + 1) * 64], ke[h][:, cols], qe[h][:, cols],
                                              start=True, stop=True), reads=[keb[h], qeb[h]], writes=[Pb[1]])
            C.op("dve", lambda e: e.tensor_tensor(out=att[:], in0=P[1][0:64, 0:128], in1=maskU2[:], op=ALU.mult),
                 reads=[Pb[1], mkb], writes=[attb])
            po = 2 + (nchunk % 2)
            for h in range(2):
                C.op("pe", lambda e: e.matmul(P[po][0:64, h * 128:(h + 1) * 128], att[:, h * 64:(h + 1) * 64],
                                              vt[:, c, h * 128:(h + 1) * 128], start=True, stop=False),
                     reads=[attb, vtb], writes=[Pb[po]])
                C.op("pe", lambda e: e.matmul(P[po][0:64, h * 128:(h + 1) * 128], qe[h][:, cols], Sbf[:, h, :],
                                              start=False, stop=True), reads=[qeb[h], Sbfb], writes=[Pb[po]])
            for h in range(2):
                C.op("pe", lambda e: e.matmul(P[4][0:64, h * 128:(h + 1) * 128], ketm[:, h * 64:(h + 1) * 64],
                                              vt[:, c, h * 128:(h + 1) * 128], start=True, stop=True),
                     reads=[ketmb, vtb], writes=[Pb[4]])
            C.op("dve", lambda e: e.tensor_tensor(out=tmpS[:].rearrange("p h d -> p (h d)"),
                                                  in0=Sf[:].rearrange("p h d -> p (h d)"), in1=P[4][0:64, 0:256], op=ALU.add),
                 reads=[Sfb, Pb[4]], writes=[tmpSb])
            for h in range(2):
                C.op("dve", lambda e: e.tensor_scalar(out=Sf[:, h, :], in0=tmpS[:, h, :],
                                                      scalar1=eP[h][:, c * CH + CH - 1:c * CH + CH], scalar2=None, op0=ALU.mult),
                     reads=[tmpSb, ePb[h]], writes=[Sfb])
            C.op("act", lambda e: e.copy(out=Sbf[:], in_=Sf[:]), reads=[Sfb], writes=[Sbfb])
            for h in range(2):
                C.op("act", lambda e: e.activation(out=junk[:], in_=P[po][0:64, h * 128:(h + 1) * 128], func=AF.Square,
                                                   accum_out=st[:, h:h + 1]), reads=[Pb[po]], writes=[junkb, stb])
            C.op("act", lambda e: e.activation(out=st[:, 2:4], in_=st[:, 0:2], func=AF.Ln, scale=1.0 / DV, bias=EPS),
                 reads=[stb], writes=[stb])
            C.op("act", lambda e: e.activation(out=st[:, 4:6], in_=st[:, 2:4], func=AF.Exp, scale=-0.5), reads=[stb], writes=[stb])
            for h in range(2):
                C.op("dve", lambda e: e.scalar_tensor_tensor(out=ytm[:, h * 128:(h + 1) * 128], in0=P[po][0:64, h * 128:(h + 1) * 128],
                                                             scalar=st[:, 4 + h:5 + h], in1=nwg[:, c, h * 128:(h + 1) * 128],
                                                             op0=ALU.mult, op1=ALU.mult),
                     reads=[Pb[po], stb, nwgb], writes=[ytmb])
            for h in range(2):
                C.op("pe", lambda e: e.matmul(P[5][:, 256 + h * 64:256 + (h + 1) * 64], ytm[:, h * 128:(h + 1) * 128],
                                              S.ident[0:64, 0:64], start=True, stop=True),
                     reads=[ytmb, S.idb], writes=[Pb[5]])
            for h in range(2):
                C.op("act", lambda e: e.copy(out=yst[h][:, cols], in_=P[5][:, 256 + h * 64:256 + (h + 1) * 64]),
                     reads=[Pb[5]], writes=[ystb[h]])
            nchunk += 1
        for h in range(2):
            C.dma("sp", S.yT[h * 128:(h + 1) * 128, tok0:tok0 + TT], yst[h][:], reads=[ystb[h]], writes=[S.yTb[0]])
    C.pop()


def gla_inputs(W, hh):
    h0 = 2 * hh
    c = np.zeros((64, 4), np.float32)
    c[:, 0] = W["gla_b_lr"][h0 * 64:h0 * 64 + 64]; c[:, 1] = W["gla_b_lr"][h0 * 64 + 64:h0 * 64 + 128]
    cm = np.ones((64, TT), np.float32); cm[:, ::64] = 0.0
    s = np.arange(64)
    m = (s[:, None] <= s[None, :]).astype(np.float32)
    return {"gla_wlr": np.ascontiguousarray(W["gla_w_lr"][:, h0 * 64:h0 * 64 + 128]), "gla_c": c,
            "gla_nw2": np.ascontiguousarray(np.broadcast_to(np.tile(W["gla_norm"], 2)[None, :], (64, 256))).astype(np.float32),
            "gla_cmask": cm, "gla_mask": np.concatenate([m, m], axis=1).astype(np.float32)}


def gdn_mixer(S):
    C = S.C
    C.push()
    P, Pb, PT, PTb = S.P, S.Pb, S.PT, S.PTb
    DH = 128
    NCH = 8
    cw_d = C.dram("gdn_cw", [128, 6, 4], F32, kind="ExternalInput")
    gc_d = C.dram("gdn_c", [128, 8], F32, kind="ExternalInput")
    selS_d = C.dram("gdn_selS", [128, 2, 128], F32, kind="ExternalInput")
    selR_d = C.dram("gdn_selR", [128, 4, 128], F32, kind="ExternalInput")
    D2_d = C.dram("gdn_D2", [128, 64], F32, kind="ExternalInput")
    mk_d = C.dram("gdn_masks", [128, 3, 128], F32, kind="ExternalInput")
    ones_d = C.dram("gdn_ones", [128, 128], F32, kind="ExternalInput")
    cm_d = C.dram("gdn_cmask", [128, 1024], F32, kind="ExternalInput")
    nw_d = C.dram("gdn_nwb", [128, 128], F32, kind="ExternalInput")
    cw, cwb = load_const(C, "gdn_cw", [128, 6, 4], F32, cw_d[:, :, :])
    gcn, gcb = load_const(C, "gdn_c", [128, 8], F32, gc_d[:, :])
    selS, selSb = load_const(C, "gdn_selS", [128, 2, 128], F32, selS_d[:, :, :])
    selR, selRb = load_const(C, "gdn_selR", [128, 4, 128], F32, selR_d[:, :, :])
    D2, D2b = load_const(C, "gdn_D2", [128, 64], F32, D2_d[:, :])
    masks, mkb = load_const(C, "gdn_masks", [128, 3, 128], F32, mk_d[:, :, :])
    onesf, onesb = load_const(C, "gdn_ones", [128, 128], F32, ones_d[:, :])
    cmask, cmb = load_const(C, "gdn_cmask", [128, 1024], F32, cm_d[:, :])
    nwb, nwbb = load_const(C, "gdn_nwb", [128, 128], F32, nw_d[:, :])
    identf, idfb, ident, idb = S.identf, S.idfb, S.ident, S.idb
    negA = C.sbuf("gdn_negA", [128, 1], F32); negAb = Buf()
    C.op("dve", lambda e: e.memset(negA[:], 0.0), writes=[negAb])
    C.op("act", lambda e: e.activation(out=negA[64:66, :], in_=gcn[64:66, 1:2], func=AF.Exp), reads=[gcb, negAb], writes=[negAb])
    C.op("dve", lambda e: e.tensor_scalar(out=negA[64:66, :], in0=negA[64:66, :], scalar1=-1.0, scalar2=None, op0=ALU.mult),
         reads=[negAb], writes=[negAb])
    t_ = lambda nm, shp, dt: (C.sbuf("gdn_" + nm, shp, dt), Buf(nm))
    Sf = [t_(f"S{h}", [128, DH], F32) for h in range(2)]
    Sbf = [t_(f"Sbf{h}", [128, DH], BF16) for h in range(2)]
    for h in range(2):
        C.op("dve", lambda e: e.memset(Sf[h][0][:], 0.0), writes=[Sf[h][1]])
        C.op("dve", lambda e: e.memset(Sbf[h][0][:], 0.0), writes=[Sbf[h][1]])
    xin, xinb = t_("xin", [128, 3 + TT], F32)
    acc, accb = t_("acc", [128, TT], F32)
    cv, cvb = t_("cv", [128, TT], F32)
    sq, sqb = t_("sq", [128, TT], F32)
    rs, rsb = t_("rs", [128, TT], F32)
    kcat, kcatb = t_("kcat", [128, NCH, 2, 64], BF16)
    qcat, qcatb = t_("qcat", [128, NCH, 2, 64], BF16)
    vcat, vcatb = t_("vcat", [128, NCH, 2, 64], BF16)
    rowt, rowtb = t_("rowt", [128, TT], F32)
    BR, BRb = t_("BR", [128, NCH, 2, 64], F32)
    GR, GRb = t_("GR", [128, NCH, 2, 64], F32)
    GC, GCb = t_("GC", [128, NCH, 2, 64], F32)
    eGC, eGCb = t_("eGC", [128, NCH, 2, 64], F32)
    bs, bsb = t_("bs", [128, TT], F32)
    gs, gsb = t_("gs", [128, TT], F32)
    gcs, gcsb = t_("gcs", [128, TT], F32)
    tmpd, tmpdb = t_("tmpd", [128, NCH, 64], F32)
    bcol, bcolb = t_("bcol", [128, NCH], F32)
    gccol, gccolb = t_("gccol", [128, NCH], F32)
    ekd, ekdb = t_("ekd", [128, NCH], F32)
    eglR, eglRb = t_("eglR", [128, NCH, 2], F32)
    gtm, gtmb = t_("gtm", [128, NCH, DH], F32)
    nwg, nwgb = t_("nwg", [128, NCH, DH], F32)
    kb, kbb = t_("kb", [128, 128], BF16)
    padA, padAb = t_("padA", [128, 128], BF16); padB, padBb = t_("padB", [128, 128], BF16)
    qpadA, qpadAb = t_("qpadA", [128, 128], BF16); qpadB, qpadBb = t_("qpadB", [128, 128], BF16)
    for tl, tb in ((padA, padAb), (padB, padBb), (qpadA, qpadAb), (qpadB, qpadBb)):
        C.op("dve", lambda e: e.memset(tl[:], 0.0), writes=[tb])
    kdtm, kdtmb = t_("kdtm", [128, 128], BF16)
    vbt, vbtb = t_("vbt", [128, 128], F32)
    dL, dLb = t_("dL", [128, 128], F32); dU, dUb = t_("dU", [128, 128], F32)
    eLm, eLmb = t_("eLm", [128, 128], F32); eUs, eUsb = t_("eUs", [128, 128], F32); eUi, eUib = t_("eUi", [128, 128], F32)
    Uf, Ufb = t_("Uf", [128, 128], F32)
    Lq = [t_(f"Lq{i}", [128, 128], BF16) for i in range(2)]
    UY = [t_(f"UY{i}", [128, 256], BF16) for i in range(2)]
    Yf, Yfb = t_("Yf", [128, 128], F32)
    TTm, TTmb = t_("TT", [128, 128], BF16)
    attn, attnb = t_("attn", [128, 128], BF16)
    rt, rtb = t_("r", [128, 128], BF16)
    vnew, vnewb = t_("vnew", [128, 128], BF16)
    st, stb = t_("st", [128, 4], F32)
    junk, junkb = t_("junk", [128, 128], F32)
    ytm, ytmb = t_("ytm", [128, 128], BF16)
    yst = [t_(f"yst{h}", [128, TT], BF16) for h in range(2)]
    flat = lambda tl: tl[:].rearrange("p c h j -> p (c h j)")

    for t in range(NTILE):
        tok0 = t * TT
        for ci in range(6):
            if t == 0:
                C.op("dve", lambda e: e.memset(xin[:, 0:3], 0.0), writes=[xinb])
                C.dma("sp", xin[:, 3:3 + TT], S.projFM[2 + ci, :, 0:TT], reads=[S.fmb[2 + ci]], writes=[xinb])
            else:
                C.dma("sp", xin[:, :], S.projFM[2 + ci, :, tok0 - 3:tok0 + TT], reads=[S.fmb[2 + ci]], writes=[xinb])
            C.op("dve", lambda e: e.tensor_scalar(out=acc[:], in0=xin[:, 0:TT], scalar1=cw[:, ci, 0:1], scalar2=None, op0=ALU.mult),
                 reads=[xinb, cwb], writes=[accb])
            for j in range(1, 4):
                C.op("dve", lambda e: e.scalar_tensor_tensor(out=acc[:], in0=xin[:, j:j + TT], scalar=cw[:, ci, j:j + 1], in1=acc[:],
                                                             op0=ALU.mult, op1=ALU.add), reads=[xinb, cwb, accb], writes=[accb])
            C.op("act", lambda e: e.activation(out=cv[:], in_=acc[:], func=AF.Silu), reads=[accb], writes=[cvb])
            h = ci % 2
            if ci < 4:
                C.op("act", lambda e: e.activation(out=sq[:], in_=cv[:], func=AF.Square), reads=[cvb], writes=[sqb])
                C.op("pe", lambda e: e.matmul(P[0][:], onesf[:], sq[:], start=True, stop=True), reads=[onesb, sqb], writes=[Pb[0]])
                C.op("act", lambda e: e.activation(out=rs[:], in_=P[0][:], func=AF.Ln, bias=EPS), reads=[Pb[0]], writes=[rsb])
                C.op("act", lambda e: e.activation(out=rs[:], in_=rs[:], func=AF.Exp, scale=-0.5), reads=[rsb], writes=[rsb])
                cvv = cv[:].rearrange("p (c j) -> p c j", j=64)
                rsv = rs[:].rearrange("p (c j) -> p c j", j=64)
                if ci < 2:
                    C.op("dve", lambda e: e.scalar_tensor_tensor(out=qcat[:, :, h, :], in0=cvv, scalar=float(DH) ** -0.5, in1=rsv,
                                                                 op0=ALU.mult, op1=ALU.mult), reads=[cvb, rsb], writes=[qcatb])
                else:
                    C.op("dve", lambda e: e.tensor_tensor(out=kcat[:, :, h, :], in0=cvv, in1=rsv, op=ALU.mult),
                         reads=[cvb, rsb], writes=[kcatb])
            else:
                C.op("dve", lambda e: e.tensor_copy(out=vcat[:, :, h, :], in_=cv[:].rearrange("p (c j) -> p c j", j=64)),
                     reads=[cvb], writes=[vcatb])
        C.dma("sp", rowt[:], S.projFM[14, :, tok0:tok0 + TT], reads=[S.fmb[14]], writes=[rowtb])
        C.op("act", lambda e: e.activation(out=rowt[32:34, :], in_=rowt[32:34, :], func=AF.Sigmoid), reads=[rowtb], writes=[rowtb])
        C.op("act", lambda e: e.activation(out=rowt[64:66, :], in_=rowt[64:66, :], func=AF.Exp, bias=gcn[64:66, 0:1]),
             reads=[rowtb, gcb], writes=[rowtb])
        C.op("act", lambda e: e.activation(out=rowt[64:66, :], in_=rowt[64:66, :], func=AF.Ln, bias=1.0), reads=[rowtb], writes=[rowtb])
        C.op("dve", lambda e: e.tensor_scalar(out=rowt[64:66, :], in0=rowt[64:66, :], scalar1=negA[64:66, 0:1], scalar2=None,
                                              op0=ALU.mult), reads=[rowtb, negAb], writes=[rowtb])
        for hq in range(2):
            C.op("pe", lambda e: e.matmul(P[1][:], selR[:, hq, :], rowt[:], start=True, stop=True), reads=[selRb, rowtb], writes=[Pb[1]])
            C.op("act", lambda e: e.copy(out=BR[:, :, hq, :], in_=P[1][:].rearrange("p (c j) -> p c j", j=64)),
                 reads=[Pb[1]], writes=[BRb])
            C.op("pe", lambda e: e.matmul(P[2][:], selR[:, 2 + hq, :], rowt[:], start=True, stop=True), reads=[selRb, rowtb], writes=[Pb[2]])
            C.op("act", lambda e: e.copy(out=GR[:, :, hq, :], in_=P[2][:].rearrange("p (c j) -> p c j", j=64)),
                 reads=[Pb[2]], writes=[GRb])
        C.op("dve", lambda e: e.tensor_tensor_scan(out=flat(GC), data0=cmask[:], data1=flat(GR), initial=0.0, op0=ALU.mult, op1=ALU.add),
             reads=[GRb, cmb], writes=[GCb])
        C.op("act", lambda e: e.activation(out=flat(eGC), in_=flat(GC), func=AF.Exp), reads=[GCb], writes=[eGCb])
        C.op("act", lambda e: e.activation(out=eglR[:], in_=GC[:, :, :, 63], func=AF.Exp), reads=[GCb], writes=[eglRb])
        C.op("pe", lambda e: e.matmul(P[1][:], selS[:, 0, :], rowt[:], start=True, stop=True), reads=[selSb, rowtb], writes=[Pb[1]])
        C.op("act", lambda e: e.copy(out=bs[:], in_=P[1][:]), reads=[Pb[1]], writes=[bsb])
        C.op("pe", lambda e: e.matmul(P[2][:], selS[:, 1, :], rowt[:], start=True, stop=True), reads=[selSb, rowtb], writes=[Pb[2]])
        C.op("act", lambda e: e.copy(out=gs[:], in_=P[2][:]), reads=[Pb[2]], writes=[gsb])
        C.op("dve", lambda e: e.tensor_tensor_scan(out=gcs[:], data0=cmask[:, 0:TT], data1=gs[:], initial=0.0, op0=ALU.mult, op1=ALU.add),
             reads=[gsb, cmb], writes=[gcsb])
        D2v = D2[:].unsqueeze(1).to_broadcast([128, NCH, 64])
        C.op("dve", lambda e: e.tensor_tensor(out=tmpd[:], in0=bs[:].rearrange("p (c j) -> p c j", j=64), in1=D2v, op=ALU.mult),
             reads=[bsb, D2b], writes=[tmpdb])
        C.op("dve", lambda e: e.tensor_reduce(out=bcol[:], in_=tmpd[:], axis=AX.X, op=ALU.add), reads=[tmpdb], writes=[bcolb])
        C.op("dve", lambda e: e.tensor_tensor(out=tmpd[:], in0=gcs[:].rearrange("p (c j) -> p c j", j=64), in1=D2v, op=ALU.mult),
             reads=[gcsb, D2b, tmpdb], writes=[tmpdb])
        C.op("dve", lambda e: e.tensor_reduce(out=gccol[:], in_=tmpd[:], axis=AX.X, op=ALU.add), reads=[tmpdb], writes=[gccolb])
        C.op("dve", lambda e: e.tensor_tensor(out=ekd[:], in0=gcs[:].rearrange("p (c j) -> p c j", j=64)[:, :, 63], in1=gccol[:],
                                              op=ALU.subtract), reads=[gcsb, gccolb], writes=[ekdb])
        C.op("act", lambda e: e.activation(out=ekd[:], in_=ekd[:], func=AF.Exp), reads=[ekdb], writes=[ekdb])
        for h in range(2):
            C.dma("sp", gtm[h * 64:(h + 1) * 64, :, :],
                  S.projTM[tok0:tok0 + TT, 512 + h * 128:512 + (h + 1) * 128].rearrange("(c p) d -> p c d", p=64),
                  reads=[S.tmb], writes=[gtmb])
        C.op("act", lambda e: e.activation(out=gtm[:], in_=gtm[:], func=AF.Silu), reads=[gtmb], writes=[gtmb])
        C.op("dve", lambda e: e.tensor_tensor(out=nwg[:], in0=gtm[:], in1=nwb[:].unsqueeze(1).to_broadcast([128, NCH, DH]),
                                              op=ALU.mult), reads=[gtmb, nwbb], writes=[nwgb])
        for c in range(NCH):
            kc = kcat[:, c].rearrange("p h j -> p (h j)")
            qc = qcat[:, c].rearrange("p h j -> p (h j)")
            vc = vcat[:, c].rearrange("p h j -> p (h j)")
            BRc = BR[:, c].rearrange("p h j -> p (h j)")
            GCc = GC[:, c].rearrange("p h j -> p (h j)")
            C.op("dve", lambda e: e.tensor_tensor(out=kb[:], in0=kc, in1=BRc, op=ALU.mult), reads=[kcatb, BRb], writes=[kbb])
            C.op("dve", lambda e: e.tensor_tensor(out=padA[:, 0:64], in0=kb[:, 0:64], in1=eGC[:, c, 0, :], op=ALU.mult),
                 reads=[kbb, eGCb], writes=[padAb])
            C.op("dve", lambda e: e.tensor_tensor(out=padB[:, 64:128], in0=kb[:, 64:128], in1=eGC[:, c, 1, :], op=ALU.mult),
                 reads=[kbb, eGCb], writes=[padBb])
            C.op("dve", lambda e: e.tensor_tensor(out=qpadA[:, 0:64], in0=qcat[:, c, 0, :], in1=eGC[:, c, 0, :], op=ALU.mult),
                 reads=[qcatb, eGCb], writes=[qpadAb])
            C.op("dve", lambda e: e.tensor_tensor(out=qpadB[:, 64:128], in0=qcat[:, c, 1, :], in1=eGC[:, c, 1, :], op=ALU.mult),
                 reads=[qcatb, eGCb], writes=[qpadBb])
            C.op("pe", lambda e: e.transpose(PT[0][:, 0:128], kc, ident[:]), reads=[kcatb, idb], writes=[PTb[0]])
            C.op("pe", lambda e: e.transpose(PT[0][:, 128:256], vc, ident[:]), reads=[vcatb, idb], writes=[PTb[0]])
            C.op("act", lambda e: e.activation(out=kdtm[:], in_=PT[0][:, 0:128], func=AF.Copy, scale=ekd[:, c:c + 1]),
                 reads=[PTb[0], ekdb], writes=[kdtmb])
            C.op("act", lambda e: e.activation(out=vbt[:], in_=PT[0][:, 128:256], func=AF.Copy, scale=bcol[:, c:c + 1]),
                 reads=[PTb[0], bcolb], writes=[vbtb])
            C.op("pe", lambda e: e.matmul(P[0][:, 0:128], kb[:], kc, start=True, stop=True), reads=[kbb, kcatb], writes=[Pb[0]])
            C.op("pe", lambda e: e.matmul(P[0][:, 128:256], kc, kb[:], start=True, stop=True), reads=[kbb, kcatb], writes=[Pb[0]])
            C.op("pe", lambda e: e.matmul(P[0][:, 256:384], kc, qc, start=True, stop=True), reads=[qcatb, kcatb], writes=[Pb[0]])
            C.op("dve", lambda e: e.tensor_scalar(out=dL[:], in0=GCc, scalar1=gccol[:, c:c + 1], scalar2=0.0, op0=ALU.subtract, op1=ALU.max),
                 reads=[GCb, gccolb], writes=[dLb])
            C.op("dve", lambda e: e.tensor_scalar(out=dU[:], in0=GCc, scalar1=gccol[:, c:c + 1], scalar2=0.0, op0=ALU.subtract, op1=ALU.min),
                 reads=[GCb, gccolb], writes=[dUb])
            C.op("act", lambda e: e.activation(out=dL[:], in_=dL[:], func=AF.Exp, scale=-1.0), reads=[dLb], writes=[dLb])
            C.op("act", lambda e: e.activation(out=dU[:], in_=dU[:], func=AF.Exp), reads=[dUb], writes=[dUb])
            C.op("dve", lambda e: e.tensor_tensor(out=eLm[:], in0=dL[:], in1=masks[:, 0, :], op=ALU.mult), reads=[dLb, mkb], writes=[eLmb])
            C.op("dve", lambda e: e.tensor_tensor(out=eUs[:], in0=dU[:], in1=masks[:, 1, :], op=ALU.mult), reads=[dUb, mkb], writes=[eUsb])
            C.op("dve", lambda e: e.tensor_tensor(out=eUi[:], in0=dU[:], in1=masks[:, 2, :], op=ALU.mult), reads=[dUb, mkb], writes=[eUib])
            L0, L0b = Lq[0]
            UY0, UY0b = UY[0]
            C.op("dve", lambda e: e.tensor_tensor(out=L0[:], in0=P[0][:, 0:128], in1=eLm[:], op=ALU.mult), reads=[Pb[0], eLmb], writes=[L0b])
            C.op("dve", lambda e: e.tensor_tensor(out=Uf[:], in0=P[0][:, 128:256], in1=eUs[:], op=ALU.mult), reads=[Pb[0], eUsb], writes=[Ufb])
            C.op("dve", lambda e: e.tensor_tensor(out=attn[:], in0=P[0][:, 256:384], in1=eUi[:], op=ALU.mult), reads=[Pb[0], eUib], writes=[attnb])
            C.op("act", lambda e: e.copy(out=UY0[:, 0:128], in_=Uf[:]), reads=[Ufb], writes=[UY0b])
            C.op("dve", lambda e: e.tensor_tensor(out=Yf[:], in0=identf[:], in1=Uf[:], op=ALU.subtract), reads=[idfb, Ufb], writes=[Yfb])
            cur = 0
            L1, L1b = Lq[1]; UY1, UY1b = UY[1]
            C.op("pe", lambda e: e.matmul(P[1][:, 0:128], L0[:], UY0[:, 0:128], start=True, stop=True), reads=[L0b, UY0b], writes=[Pb[1]])
            C.op("pe", lambda e: e.matmul(P[2][:, 0:128], UY0[:, 0:128], L0[:], start=True, stop=True), reads=[L0b, UY0b], writes=[Pb[2]])
            C.op("act", lambda e: e.copy(out=UY1[:, 0:128], in_=P[1][:, 0:128]), reads=[Pb[1]], writes=[UY1b])
            C.op("act", lambda e: e.copy(out=UY1[:, 128:256], in_=Yf[:]), reads=[Yfb], writes=[UY1b])
            C.op("act", lambda e: e.copy(out=L1[:], in_=P[2][:, 0:128]), reads=[Pb[2]], writes=[L1b])
            cur = 1
            for lev in range(4):
                Lc, Lcb = Lq[cur]; UYc, UYcb = UY[cur]
                Ln_, Lnb = Lq[1 - cur]; UYn, UYnb = UY[1 - cur]
                C.op("pe", lambda e: e.matmul(P[1][:, 0:256], Lc[:], UYc[:, 0:256], start=True, stop=True), reads=[Lcb, UYcb], writes=[Pb[1]])
                C.op("pe", lambda e: e.matmul(P[2][:, 0:128], UYc[:, 0:128], Lc[:], start=True, stop=True), reads=[Lcb, UYcb], writes=[Pb[2]])
                C.op("dve", lambda e: e.tensor_tensor(out=Yf[:], in0=Yf[:], in1=P[1][:, 128:256], op=ALU.add), reads=[Yfb, Pb[1]], writes=[Yfb])
                C.op("act", lambda e: e.copy(out=UYn[:, 0:128], in_=P[1][:, 0:128]), reads=[Pb[1]], writes=[UYnb])
                C.op("act", lambda e: e.copy(out=UYn[:, 128:256], in_=Yf[:]), reads=[Yfb], writes=[UYnb])
                C.op("act", lambda e: e.copy(out=Ln_[:], in_=P[2][:, 0:128]), reads=[Pb[2]], writes=[Lnb])
                cur = 1 - cur
            Lc, Lcb = Lq[cur]; UYc, UYcb = UY[cur]
            C.op("pe", lambda e: e.matmul(P[1][:, 0:128], Lc[:], UYc[:, 128:256], start=True, stop=True), reads=[Lcb, UYcb], writes=[Pb[1]])
            C.op("dve", lambda e: e.tensor_tensor(out=TTm[:], in0=Yf[:], in1=P[1][:, 0:128], op=ALU.add), reads=[Yfb, Pb[1]], writes=[TTmb])
            C.op("pe", lambda e: e.matmul(P[3][:, 0:128], padA[:], Sbf[0][0][:], start=True, stop=False), reads=[padAb, Sbf[0][1]], writes=[Pb[3]])
            C.op("pe", lambda e: e.matmul(P[3][:, 0:128], padB[:], Sbf[1][0][:], start=False, stop=True), reads=[padBb, Sbf[1][1]], writes=[Pb[3]])
            C.op("dve", lambda e: e.tensor_tensor(out=rt[:], in0=vbt[:], in1=P[3][:, 0:128], op=ALU.subtract), reads=[vbtb, Pb[3]], writes=[rtb])
            C.op("pe", lambda e: e.matmul(P[3][:, 128:256], TTm[:], rt[:], start=True, stop=True), reads=[TTmb, rtb], writes=[Pb[3]])
            C.op("act", lambda e: e.copy(out=vnew[:], in_=P[3][:, 128:256]), reads=[Pb[3]], writes=[vnewb])
            C.op("pe", lambda e: e.matmul(P[3][:, 256:384], qpadA[:], Sbf[0][0][:], start=True, stop=False), reads=[qpadAb, Sbf[0][1]], writes=[Pb[3]])
            C.op("pe", lambda e: e.matmul(P[3][:, 256:384], qpadB[:], Sbf[1][0][:], start=False, stop=False), reads=[qpadBb, Sbf[1][1]], writes=[Pb[3]])
            C.op("pe", lambda e: e.matmul(P[3][:, 256:384], attn[:], vnew[:], start=False, stop=True), reads=[attnb, vnewb], writes=[Pb[3]])
            for h in range(2):
                hp = slice(h * 64, (h + 1) * 64)
                C.op("pe", lambda e: e.matmul(P[4 + h][:, 0:128], kdtm[hp, :], vnew[hp, :], start=True, stop=True),
                     reads=[kdtmb, vnewb], writes=[Pb[4 + h]])
                C.op("dve", lambda e: e.scalar_tensor_tensor(out=Sf[h][0][:], in0=Sf[h][0][:], scalar=eglR[:, c, h:h + 1], in1=P[4 + h][:, 0:128],
                                                             op0=ALU.mult, op1=ALU.add), reads=[Sf[h][1], eglRb, Pb[4 + h]], writes=[Sf[h][1]])
                C.op("act", lambda e: e.copy(out=Sbf[h][0][:], in_=Sf[h][0][:]), reads=[Sf[h][1]], writes=[Sbf[h][1]])
            C.op("act", lambda e: e.activation(out=junk[:], in_=P[3][:, 256:384], func=AF.Square, accum_out=st[:, 0:1]),
                 reads=[Pb[3]], writes=[junkb, stb])
            C.op("act", lambda e: e.activation(out=st[:, 1:2], in_=st[:, 0:1], func=AF.Ln, scale=1.0 / DH, bias=EPS), reads=[stb], writes=[stb])
            C.op("act", lambda e: e.activation(out=st[:, 2:3], in_=st[:, 1:2], func=AF.Exp, scale=-0.5), reads=[stb], writes=[stb])
            C.op("dve", lambda e: e.scalar_tensor_tensor(out=ytm[:], in0=P[3][:, 256:384], scalar=st[:, 2:3], in1=nwg[:, c, :],
                                                         op0=ALU.mult, op1=ALU.mult), reads=[Pb[3], stb, nwgb], writes=[ytmb])
            C.op("pe", lambda e: e.transpose(PT[1][:, 0:128], ytm[:], ident[:]), reads=[ytmb, idb], writes=[PTb[1]])
            for h in range(2):
                C.op("act", lambda e: e.copy(out=yst[h][0][:, c * 64:(c + 1) * 64], in_=PT[1][:, h * 64:(h + 1) * 64]),
                     reads=[PTb[1]], writes=[yst[h][1]])
        for h in range(2):
            C.dma("sp", S.yT[256 + h * 128:256 + (h + 1) * 128, tok0:tok0 + TT], yst[h][0][:], reads=[yst[h][1]], writes=[S.yTb[1]])
    C.pop()


def gdn_inputs(W, hh):
    h0 = 2 * hh
    cw = np.zeros((128, 6, 4), np.float32)
    for i, base in enumerate((0, 512, 1024)):
        for h in range(2):
            cols = np.arange(base + (h0 + h) * 128, base + (h0 + h + 1) * 128)
            cw[:, 2 * i + h, :] = W["gdn_conv"][:, cols].T
    c = np.zeros((128, 8), np.float32)
    for h in range(2):
        c[64 + h, 0] = W["gdn_dt_bias"][h0 + h]; c[64 + h, 1] = W["gdn_a_log"][h0 + h]
    selS = np.zeros((128, 2, 128), np.float32); selR = np.zeros((128, 4, 128), np.float32)
    for h in range(2):
        selS[32 + h, 0, h * 64:(h + 1) * 64] = 1.0
        selS[64 + h, 1, h * 64:(h + 1) * 64] = 1.0
        selR[32 + h, h, :] = 1.0
        selR[64 + h, 2 + h, :] = 1.0
    i = np.arange(64)
    D2 = np.concatenate([np.eye(64), np.eye(64)], axis=0).astype(np.float32)
    blk = np.kron(np.eye(2), np.ones((64, 64)))
    ii = np.arange(128) % 64
    Ls = blk * (ii[None, :] < ii[:, None]); Us = blk * (ii[:, None] < ii[None, :]); Ui = blk * (ii[:, None] <= ii[None, :])
    masks = np.stack([Ls, Us, Ui], axis=1).astype(np.float32)
    cm = np.ones((128, 1024), np.float32); cm[:, ::64] = 0.0
    return {"gdn_cw": cw, "gdn_c": c, "gdn_selS": selS, "gdn_selR": selR, "gdn_D2": D2, "gdn_masks": masks,
            "gdn_ones": np.ones((128, 128), np.float32), "gdn_cmask": cm,
            "gdn_nwb": np.ascontiguousarray(np.broadcast_to(W["gdn_norm"][None, :], (128, 128))).astype(np.float32)}


def build_mod():
    C = Ctx()
    NCOL = 3072
    cT_d = C.dram("cT", [128, NK, 4], F32, kind="ExternalInput")
    w_d = C.dram("w_mod_c", [D, NCOL], F32, kind="ExternalInput")
    b_d = C.dram("b_mod_c", [128, NCOL], F32, kind="ExternalInput")
    out_d = C.dram("mod_out", [4, NCOL], F32, kind="ExternalOutput")
    ob = Buf("mod_out")
    cT, cTb = load_const(C, "cT", [128, NK, 4], F32, cT_d[:, :, :])
    bm, bmb = load_const(C, "bm", [128, NCOL], F32, b_d[:, :])
    cond = C.sbuf("cond", [128, NK, 4], F32); condb = Buf()
    C.op("act", lambda e: e.activation(out=cond[:], in_=cT[:], func=AF.Silu), reads=[cTb], writes=[condb])
    condr = C.sbuf("condr", [128, 4, NK, 128], BF16); condrb = Buf()
    for b in range(4):
        for k in range(NK):
            C.op("dve", lambda e: e.tensor_copy(out=condr[:, b, k, :], in_=cond[:, k, b:b + 1].to_broadcast([128, 128])),
                 reads=[condb], writes=[condrb])
    NW = NCOL // 512
    ws = [C.sbuf(f"ws{i}", [128, NK, 512], BF16) for i in range(NW)]; wsb = [Buf() for _ in range(NW)]
    P = [C.psum(f"P{i}", [128, 512], F32) for i in range(2)]; Pb = [Buf() for _ in range(2)]
    res = [C.sbuf(f"res{i}", [128, NCOL], F32) for i in range(2)]; resb = [Buf() for _ in range(2)]
    wv = w_d.rearrange("(k p) c -> p k c", p=128)
    for n in range(NW):
        C.dma("pool", ws[n][:], wv[:, :, n * 512:(n + 1) * 512], writes=[wsb[n]])
    cnt = 0
    for b in range(4):
        r = res[b % 2]; rb = resb[b % 2]
        for n in range(NW):
            i = cnt % 2; cnt += 1
            for k in range(NK):
                C.op("pe", lambda e: e.matmul(P[i][:], condr[:, b, k, :], ws[n][:, k, :], start=(k == 0), stop=(k == NK - 1)),
                     reads=[condrb, wsb[n]], writes=[Pb[i]])
            C.op("dve", lambda e: e.tensor_tensor(out=r[:, n * 512:(n + 1) * 512], in0=P[i][:], in1=bm[:, n * 512:(n + 1) * 512],
                                                  op=ALU.add), reads=[Pb[i], bmb], writes=[rb])
        C.dma("sp", out_d[b:b + 1, :], r[0:1, :], reads=[rb], writes=[ob])
    C.finish([ob])
    return C.nc


def build_stage_a():
    S = StageA({"inproj"})
    S.inproj()
    pool_mixer(S)
    fox_mixer(S)
    gla_mixer(S)
    gdn_mixer(S)
    return S.finish()


_CACHE = {}


def _get(name, fn):
    if name not in _CACHE:
        _CACHE[name] = fn()
    return _CACHE[name]


def kernel(**inputs):
    inp = {k: np.asarray(v) for k, v in inputs.items()}
    x = np.ascontiguousarray(inp["x"], dtype=np.float32)
    B, Tn, Dn = x.shape
    L = inp["w_mod"].shape[0]
    cores = list(range(8))
    cT = np.ascontiguousarray(inp["c"].T.reshape(NK, 128, B).transpose(1, 0, 2)).astype(np.float32)
    in_maps = []
    for i in cores:
        l, q = i // 4, i % 4
        in_maps.append({"cT": cT, "w_mod_c": np.ascontiguousarray(inp["w_mod"][l][:, q * 3072:(q + 1) * 3072]),
                        "b_mod_c": np.ascontiguousarray(np.broadcast_to(inp["b_mod"][l][None, q * 3072:(q + 1) * 3072], (128, 3072)))})
    res = run_bass_kernel_spmd(_get("mod", build_mod), in_maps, core_ids=cores)
    mod = np.zeros((L, B, 6 * Dn), np.float32)
    for i in cores:
        l, q = i // 4, i % 4
        mod[l][:, q * 3072:(q + 1) * 3072] = res.results[i]["mod_out"]
    mod = mod.reshape(L, B, 6, Dn)
    skip = ("x", "c", "norm_final", "w_mod", "b_mod")
    for l in range(L):
        W = {k: inp[k][l] for k in inp if k not in skip}
        in_maps = []
        for i in cores:
            b, hh = i // 2, i % 2
            m = stage_a_base_inputs(x[b], W, mod[l, b], hh)
            m.update(pool_inputs(W, hh)); m.update(fox_inputs(W, hh)); m.update(gla_inputs(W, hh)); m.update(gdn_inputs(W, hh))
            in_maps.append(m)
        res = run_bass_kernel_spmd(_get("A", build_stage_a), in_maps, core_ids=cores)
        yT = np.zeros((B, 2048, Tn), NPBF)
        for i in cores:
            b, hh = i // 2, i % 2
            y = res.results[i]["yT"]
            for mx in range(4):
                yT[b, mx * 512 + hh * 256:mx * 512 + (hh + 1) * 256] = y[mx * 256:(mx + 1) * 256]
        final = (l == L - 1)
        in_maps = []
        for i in cores:
            b, half = i // 2, i % 2
            t0 = half * 2048
            xt = np.zeros((128 + 2048, Dn), np.float32); yt = np.zeros((2048, 128 + 2048), NPBF)
            if half == 0:
                xt[128:] = x[b, 0:2048]; yt[:, 128:] = yT[b][:, 0:2048]
            else:
                xt[:] = x[b, t0 - 128:t0 + 2048]; yt[:] = yT[b][:, t0 - 128:t0 + 2048]
            in_maps.append(stage_b_inputs(xt, yt, W, mod[l, b], inp["norm_final"], float(half)))
        res = run_bass_kernel_spmd(_get("B%d" % int(final), lambda: build_stage_b(final, 4)), in_maps, core_ids=cores)
        xn = np.zeros_like(x)
        for i in cores:
            b, half = i // 2, i % 2
            xn[b, half * 2048:(half + 1) * 2048] = res.results[i]["x_out"]
        x = xn
    return x
```
